# Optimizing a Trainium2 kernel written in Bass

```python
import jax, jax.numpy as jnp
from jax import lax
import numpy as np

D_MODEL = 1024
BATCH = 8
SEQ = 2048
DEPTH = 4

N_META = 16
D_LRU = D_MODEL // 2
LRU_BLOCKS = 8
LRU_BLOCK_DIM = D_LRU // LRU_BLOCKS
CONV_WIDTH = 4
LRU_C = 8.0
D_RET = D_MODEL // 2
RET_HEADS = 4
RET_HEAD_DIM = D_RET // RET_HEADS
RET_CHUNK = 128
ROPE_BASE = 10000.0
D_MIX = D_LRU + D_RET
D_IN = 2 * D_LRU + 4 * D_RET
D_FF = -(-8 * D_MODEL // (3 * 256)) * 256
EPS = 1e-6

kernel_name = "hymba_rglru_retention_swiglu"


def rmsnorm(x, gain):
    x32 = x.astype(jnp.float32)
    y = x32 * lax.rsqrt(jnp.mean(x32 * x32, axis=-1, keepdims=True) + EPS)
    return (y * gain.astype(jnp.float32)).astype(x.dtype)


def rope(x, pos):
    half = x.shape[-1] // 2
    inv = ROPE_BASE ** (-jnp.arange(half, dtype=jnp.float32) / half)
    ang = pos[:, None] * inv[None, :]
    cos = jnp.cos(ang)[None, :, None, :]
    sin = jnp.sin(ang)[None, :, None, :]
    x1, x2 = x[..., :half], x[..., half:]
    return jnp.concatenate([x1 * cos - x2 * sin, x1 * sin + x2 * cos], axis=-1)


def rglru_group(xb, gate_b, conv_w, conv_b, wa, ba, wx, bx, lam, out_gain):
    B, T, _ = xb.shape
    xf = xb.astype(jnp.float32)
    xp = jnp.pad(xf, ((0, 0), (CONV_WIDTH - 1, 0), (0, 0)))
    xc = conv_b.astype(jnp.float32) + sum(
        xp[:, i:i + T, :] * conv_w[i].astype(jnp.float32) for i in range(CONV_WIDTH))
    blocks = xc.reshape(B, T, LRU_BLOCKS, LRU_BLOCK_DIM)
    r = jax.nn.sigmoid(jnp.einsum('btgi,gij->btgj', blocks, wa.astype(jnp.float32))
                       + ba.astype(jnp.float32)).reshape(B, T, D_LRU)
    ig = jax.nn.sigmoid(jnp.einsum('btgi,gij->btgj', blocks, wx.astype(jnp.float32))
                        + bx.astype(jnp.float32)).reshape(B, T, D_LRU)
    log_a = -LRU_C * r * jax.nn.softplus(-lam.astype(jnp.float32))
    a = jnp.exp(log_a)
    mult = jnp.sqrt(-jnp.expm1(2.0 * log_a))
    b = mult * (ig * xc)

    def combine(left, right):
        a1, b1 = left
        a2, b2 = right
        return a1 * a2, a2 * b1 + b2

    _, h = lax.associative_scan(combine, (a, b), axis=1)
    y = h * jax.nn.gelu(gate_b.astype(jnp.float32))
    return rmsnorm(y, out_gain).astype(xb.dtype)


def retention_group(q, k, v, g, gn_gain):
    B, T, _ = q.shape
    H, d, C = RET_HEADS, RET_HEAD_DIM, RET_CHUNK
    qf = q.astype(jnp.float32).reshape(B, T, H, d)
    kf = k.astype(jnp.float32).reshape(B, T, H, d)
    vf = v.astype(jnp.float32).reshape(B, T, H, d)
    pos = jnp.arange(T, dtype=jnp.float32)
    qf = rope(qf, pos)
    kf = rope(kf, pos) * (d ** -0.5)
    pad = (-T) % C
    padw = ((0, 0), (pad, 0), (0, 0), (0, 0))
    Tp = T + pad
    N = Tp // C
    qc = jnp.pad(qf, padw).reshape(B, N, C, H, d)
    kc = jnp.pad(kf, padw).reshape(B, N, C, H, d)
    vc = jnp.pad(vf, padw).reshape(B, N, C, H, d)

    log_g = jnp.log(1.0 - 2.0 ** (-5.0 - jnp.arange(H, dtype=jnp.float32)))
    idx = jnp.arange(C, dtype=jnp.float32)
    diff = idx[:, None] - idx[None, :]
    dmask = jnp.where(diff[None] >= 0,
                      jnp.exp(jnp.maximum(diff, 0.0)[None] * log_g[:, None, None]), 0.0)
    xi = jnp.exp((idx + 1.0)[:, None] * log_g[None, :])
    zeta = jnp.exp((C - 1.0 - idx)[:, None] * log_g[None, :])
    g_chunk = jnp.exp(C * log_g)

    scores = jnp.einsum('bnchd,bnshd->bnhcs', qc, kc) * dmask[None, None]
    y_intra = jnp.einsum('bnhcs,bnshe->bnche', scores, vc)

    def step(S, xs):
        qi, ki, vi = xs
        cross = jnp.einsum('bchd,bhde->bche', qi, S) * xi[None, :, :, None]
        S = S * g_chunk[None, :, None, None] + jnp.einsum(
            'bchd,bche->bhde', ki * zeta[None, :, :, None], vi)
        return S, cross

    S0 = jnp.zeros((B, H, d, d), jnp.float32)
    _, y_cross = lax.scan(step, S0, (jnp.moveaxis(qc, 1, 0),
                                     jnp.moveaxis(kc, 1, 0),
                                     jnp.moveaxis(vc, 1, 0)))
    y = (y_intra + jnp.moveaxis(y_cross, 0, 1)).reshape(B, Tp, H, d)[:, pad:]
    mu = jnp.mean(y, axis=-1, keepdims=True)
    var = jnp.mean(jnp.square(y - mu), axis=-1, keepdims=True)
    y = ((y - mu) * lax.rsqrt(var + EPS)).reshape(B, T, D_RET) * gn_gain.astype(jnp.float32)
    return (jax.nn.silu(g.astype(jnp.float32)) * y).astype(q.dtype)


def setup_inputs(seed: int = 0) -> dict:
    key = jax.random.key(seed)
    ks = jax.random.split(key, 24)
    f32 = jnp.float32

    def nrm(k, shape, scale):
        return jax.random.normal(k, shape, f32) * scale

    def gain(k, shape):
        return 1.0 + 0.02 * jax.random.normal(k, shape, f32)

    u = jax.random.uniform(ks[10], (DEPTH, D_LRU), f32, minval=0.9, maxval=0.999)
    s = u ** (1.0 / LRU_C)
    lru_lambda = jnp.log(s) - jnp.log1p(-s)
    return {
        "x": nrm(ks[0], (BATCH, SEQ, D_MODEL), 1.0),
        "meta_tokens": nrm(ks[1], (N_META, D_MODEL), 1.0),
        "norm_mix": gain(ks[2], (DEPTH, D_MODEL)),
        "w_in": nrm(ks[3], (DEPTH, D_MODEL, D_IN), D_MODEL ** -0.5),
        "conv_w": nrm(ks[4], (DEPTH, CONV_WIDTH, D_LRU), CONV_WIDTH ** -0.5),
        "conv_b": nrm(ks[5], (DEPTH, D_LRU), 0.01),
        "gate_a_w": nrm(ks[6], (DEPTH, LRU_BLOCKS, LRU_BLOCK_DIM, LRU_BLOCK_DIM), LRU_BLOCK_DIM ** -0.5),
        "gate_a_b": nrm(ks[7], (DEPTH, LRU_BLOCKS, LRU_BLOCK_DIM), 0.01),
        "gate_x_w": nrm(ks[8], (DEPTH, LRU_BLOCKS, LRU_BLOCK_DIM, LRU_BLOCK_DIM), LRU_BLOCK_DIM ** -0.5),
        "gate_x_b": nrm(ks[9], (DEPTH, LRU_BLOCKS, LRU_BLOCK_DIM), 0.01),
        "lru_lambda": lru_lambda,
        "lru_out_norm": gain(ks[11], (DEPTH, D_LRU)),
        "ret_out_norm": gain(ks[12], (DEPTH, D_RET)),
        "w_out": nrm(ks[13], (DEPTH, D_MIX, D_MODEL), D_MIX ** -0.5),
        "norm_ffn": gain(ks[14], (DEPTH, D_MODEL)),
        "w_gate": nrm(ks[15], (DEPTH, D_MODEL, D_FF), D_MODEL ** -0.5),
        "w_up": nrm(ks[16], (DEPTH, D_MODEL, D_FF), D_MODEL ** -0.5),
        "w_down": nrm(ks[17], (DEPTH, D_FF, D_MODEL), D_FF ** -0.5),
        "norm_final": gain(ks[18], (D_MODEL,)),
    }


def reference(x, meta_tokens, norm_mix, w_in, conv_w, conv_b, gate_a_w, gate_a_b,
              gate_x_w, gate_x_b, lru_lambda, lru_out_norm, ret_out_norm, w_out,
              norm_ffn, w_gate, w_up, w_down, norm_final):
    B = x.shape[0]
    meta = jnp.broadcast_to(meta_tokens[None].astype(x.dtype), (B, N_META, x.shape[-1]))
    h_res = jnp.concatenate([meta, x], axis=1)
    split_at = [D_LRU, 2 * D_LRU, 2 * D_LRU + D_RET, 2 * D_LRU + 2 * D_RET,
                2 * D_LRU + 3 * D_RET]
    for l in range(DEPTH):
        hn = rmsnorm(h_res, norm_mix[l])
        proj = hn @ w_in[l]
        x_lru, g_lru, q, k, v, g_ret = jnp.split(proj, split_at, axis=-1)
        y_lru = rglru_group(x_lru, g_lru, conv_w[l], conv_b[l], gate_a_w[l], gate_a_b[l],
                            gate_x_w[l], gate_x_b[l], lru_lambda[l], lru_out_norm[l])
        y_ret = retention_group(q, k, v, g_ret, ret_out_norm[l])
        h_res = h_res + jnp.concatenate([y_lru, y_ret], axis=-1) @ w_out[l]
        hn = rmsnorm(h_res, norm_ffn[l])
        h_res = h_res + (jax.nn.silu(hn @ w_gate[l]) * (hn @ w_up[l])) @ w_down[l]
    out = rmsnorm(h_res, norm_final)
    return out[:, N_META:, :]
```

```python
import contextlib
import numpy as np
import concourse.bass as bass
import concourse.mybir as mybir
from concourse.bass_utils import run_bass_kernel_spmd

F32 = mybir.dt.float32
BF16 = mybir.dt.bfloat16
AF = mybir.ActivationFunctionType
ALU = mybir.AluOpType

D = 1024
SEQ = 2048
NMETA = 16
DEPTH = 4
DFF = 2816
NFF = DFF // 128
PADF = 112
TP = 2176
NT = TP // 128
HALVES = [(0, 9), (9, 17)]
HW = 9 * 128
RS = 24
EPS = 1e-6
SELF_SYNC = True
DBG = None
DBG_OUT = {}

NM, NF, CW, CB, BA, BX, LAM, OG, RG, PL = 0, 8, 16, 32, 36, 40, 44, 48, 52, 56
HBA, HBX, P4, N4, N8, HRG, DL = 0, 4, 8, 12, 16, 20, 24
GAM = [1.0 - 2.0 ** (-5.0 - h) for h in range(4)]
FFG = [4, 4, 4, 4, 3, 3]


class Prog:
    def __init__(self):
        self.ops = {k: [] for k in ("pe", "act", "dve", "pool", "sp")}
        self.last_w = {}
        self.readers = {}
        self.dma_cnt = {}
        self.touch = {}
        self.clock = 0

    def _deps(self, reads, writes, tok):
        deps = []
        self.clock += 1
        for r in list(reads) + list(writes):
            self.touch[r] = self.clock
        for r in reads:
            t = self.last_w.get(r)
            if t is not None:
                deps.append(t)
        for w in writes:
            t = self.last_w.get(w)
            if t is not None:
                deps.append(t)
            deps.extend(self.readers.get(w, ()))
        for r in reads:
            self.readers.setdefault(r, []).append(tok)
        for w in writes:
            self.last_w[w] = tok
            self.readers[w] = []
        return deps

    def op(self, eng, fn, reads=(), writes=()):
        lst = self.ops[eng]
        tok = ("E", eng, len(lst))
        deps = self._deps(reads, writes, tok)
        lst.append({"fn": fn, "deps": deps, "tok": tok, "inc": False})

    def dma(self, q, fn, sem, reads=(), writes=()):
        self.dma_cnt[sem] = self.dma_cnt.get(sem, 0) + 16
        tok = ("D", sem, self.dma_cnt[sem])
        deps = self._deps(reads, writes, tok)
        self.ops[q].append({"fn": fn, "deps": deps, "tok": tok, "inc": False, "dsem": sem})

    def finalize(self):
        for eng, lst in self.ops.items():
            for o in lst:
                nd = []
                for d in set(o["deps"]):
                    if d[0] == "E":
                        if d[1] == eng and (eng == "pe" or not SELF_SYNC):
                            continue
                        if d == o["tok"]:
                            continue
                        self.ops[d[1]][d[2]]["inc"] = True
                    nd.append(d)
                o["deps"] = nd
        self.cnt = {}
        for eng, lst in self.ops.items():
            c = 0
            arr = []
            for o in lst:
                if o["inc"]:
                    c += 1
                arr.append(c)
            self.cnt[eng] = arr

    def emit(self, eng, e, esems, dsems):
        known = {}
        for o in self.ops[eng]:
            need = {}
            for d in o["deps"]:
                if d[0] == "E":
                    key = ("E", d[1])
                    val = self.cnt[d[1]][d[2]]
                else:
                    key = ("D", d[1])
                    val = d[2]
                if val > known.get(key, 0):
                    need[key] = max(need.get(key, 0), val)
            for key, val in need.items():
                sem = esems[key[1]] if key[0] == "E" else dsems[key[1]]
                e.wait_ge(sem, val)
                known[key] = val
            if o["fn"] is None:
                continue
            ins = o["fn"](e)
            if "dsem" in o:
                ins.then_inc(dsems[o["dsem"]], 16)
            elif o["inc"]:
                ins.then_inc(esems[eng], 1)


def build_nc():
    nc = bass.Bass("TRN2", target_bir_lowering=False)
    dt = lambda n, s: nc.dram_tensor(n, s, F32, kind="ExternalInput").ap()
    x_d = dt("x", [SEQ, D])
    meta_d = dt("meta", [NMETA, D])
    win_d = dt("w_in", [DEPTH, D, 3072])
    wout_d = dt("w_out", [DEPTH, D, D])
    wg_d = dt("w_gate", [DEPTH, D, DFF])
    wu_d = dt("w_up", [DEPTH, D, DFF])
    wd_d = dt("w_down", [DEPTH, DFF, D])
    gates_d = dt("gates", [128, DEPTH * 4 * 2 * 128])
    pp_d = dt("pp", [128, DEPTH * PL])
    cst_d = dt("cst", [128, 128 + 3 * 512])
    cs_d = dt("cs", [128, 2 * 2 * 576])
    gfin_d = dt("gfin", [128, D])
    out_d = nc.dram_tensor("out", [SEQ, D], F32, kind="ExternalOutput").ap()
    if DBG is not None:
        dbg_d = nc.dram_tensor("dbg", [128, 8 * HW], BF16, kind="ExternalOutput").ap()

    P = Prog()
    es = contextlib.ExitStack()
    with es:
        sb = lambda n, s, d=F32: es.enter_context(nc.sbuf_tensor(n, s, d))
        H = sb("H", [128, 8, HW])
        HN = sb("HN", [128, 8, HW], BF16)
        YM = sb("YM", [128, 8, HW], BF16)
        RING = sb("RING", [128, RS, 1024], BF16)
        CST = sb("CST", [128, 128 + 3 * 512])
        CS = sb("CS", [128, 2, 576])
        GATES = sb("GATES", [128, 1024], BF16)
        PP = sb("PP", [128, DEPTH * PL])
        DV = sb("DV", [128, DEPTH * DL])
        TMPS = sb("TMPS", [128, 3, 16])
        IDB = sb("IDB", [128, 128], BF16)
        ONESB = sb("ONESB", [128, 128], BF16)
        SF = sb("SF", [128, DEPTH, 512])
        SB2 = sb("SB2", [128, 2, 512], BF16)
        CAR = sb("CAR", [128, DEPTH * 4])
        HIST = sb("HIST", [128, DEPTH * 4, 4])
        XS = sb("XS", [128, 2, 520])
        WK = sb("WK", [128, 15, 512])
        BK = sb("BK", [128, 28, 512], BF16)
        STAT = sb("STAT", [128, 2, 4, 8])
        MV = sb("MV", [128, 2, 4, 2])
        RSTD = sb("RSTD", [128, 12])
        PS = es.enter_context(nc.psum_tensor("PS", [128, 6, 512], F32))
        PSB = es.enter_context(nc.psum_tensor("PSB", [128, 2, 1024], BF16))
        esems = {k: es.enter_context(nc.semaphore("s_" + k)) for k in ("pe", "act", "dve")}
        ND = RS + 9
        dsems = [es.enter_context(nc.semaphore("d%d" % i)) for i in range(ND)]
        SEM_XST = [RS, RS + 1]
        SEM_OUT = [RS + 2, RS + 3]
        SEM_SETUP = RS + 4

        IDENT = CST[:, 0:128]
        MASK = CST[:, 128:640]
        XI = CST[:, 640:1152]
        ZETA = CST[:, 1152:1664]
        GFIN = WK[:, 8:10, :].rearrange("p a f -> p (a f)")
        tlo = [0]

        def COS(gt):
            return CS[:, 0, (gt - tlo[0]) * 64:(gt - tlo[0] + 1) * 64]

        def SIN(gt):
            return CS[:, 1, (gt - tlo[0]) * 64:(gt - tlo[0] + 1) * 64]

        def act(out, in_, func, r, w, scale=None, bias=None, accum=None):
            kw = {}
            if scale is not None:
                kw["scale"] = scale
            if bias is not None:
                kw["bias"] = bias
            if accum is not None:
                kw["accum_out"] = accum
            P.op("act", lambda e: e.activation(out=out, in_=in_, func=func, **kw), r, w)

        def tt(out, in0, in1, op, r, w, eng="dve"):
            P.op(eng, lambda e: e.tensor_tensor(out=out, in0=in0, in1=in1, op=op), r, w)

        def stt(out, in0, scalar, in1, op0, op1, r, w):
            P.op("dve", lambda e: e.scalar_tensor_tensor(out=out, in0=in0, scalar=scalar, in1=in1,
                                                         op0=op0, op1=op1), r, w)

        def ts(out, in0, s1, s2, op0, op1, r, w):
            if op1 is None:
                P.op("dve", lambda e: e.tensor_scalar(out=out, in0=in0, scalar1=s1, scalar2=None, op0=op0), r, w)
            else:
                P.op("dve", lambda e: e.tensor_scalar(out=out, in0=in0, scalar1=s1, scalar2=s2,
                                                      op0=op0, op1=op1), r, w)

        def mm(out, lhsT, rhs, start, stop, r, w):
            P.op("pe", lambda e: e.matmul(out, lhsT, rhs, start=start, stop=stop), r, w)

        def tr(out, in_, ident, r, w):
            P.op("pe", lambda e: e.transpose(out, in_, ident), r, w)

        ps_busy = [False] * 6

        def ps_try(n=1, pair=False):
            tch = lambda i: P.touch.get(("PS", i), -1)
            if pair:
                c = [i for i in (0, 2, 4) if not ps_busy[i] and not ps_busy[i + 1]]
                if not c:
                    return None
                b = min(c, key=lambda i: max(tch(i), tch(i + 1)))
                ps_busy[b] = ps_busy[b + 1] = True
                P.clock += 1
                P.touch[("PS", b)] = P.touch[("PS", b + 1)] = P.clock
                return b
            free = sorted([i for i in range(6) if not ps_busy[i]], key=tch)
            if len(free) < n:
                return None
            for b in free[:n]:
                ps_busy[b] = True
                P.clock += 1
                P.touch[("PS", b)] = P.clock
            return free[0] if n == 1 else free[:n]

        def ps_free(*bs):
            for b in bs:
                ps_busy[b] = False

        def gps(n=1, pair=False):
            while True:
                r = ps_try(n, pair)
                if r is not None:
                    return r
                yield

        def ps1():
            b = ps_try(1)
            assert b is not None
            ps_busy[b] = False
            return b

        def ps2():
            b = ps_try(pair=True)
            assert b is not None
            ps_busy[b] = ps_busy[b + 1] = False
            return b

        psb_busy = [False, False]

        def gpsb():
            while True:
                for i in (0, 1):
                    if not psb_busy[i]:
                        psb_busy[i] = True
                        return i
                yield

        def psb_free(i):
            psb_busy[i] = False

        psbi = [0]

        def psb1():
            b = psbi[0] % 2
            psbi[0] += 1
            return b

        rp = [0]

        def ring_alloc(n, align=1):
            if rp[0] % align:
                rp[0] += align - rp[0] % align
            if rp[0] + n > RS:
                rp[0] = 0
            s0 = rp[0]
            rp[0] += n
            return s0

        def load_cols(w3, l, c0, s):
            src = w3[l].rearrange("(kc p) f -> p kc f", p=128)[:, :, c0:c0 + 128]
            dst = RING[:, s, :].rearrange("p (kc f) -> p kc f", kc=8)
            P.dma("pool", lambda e: e.dma_start(out=dst, in_=src), s, (), [("ring", s)])

        def load_rows(w3, l, r0, s):
            src = w3[l][r0:r0 + 128, :]
            dst = RING[:, s, :]
            P.dma("pool", lambda e: e.dma_start(out=dst, in_=src), s, (), [("ring", s)])

        P.dma("sp", lambda e: e.dma_start(out=CST[:, :], in_=cst_d), SEM_SETUP, (), ["CST"])
        P.dma("sp", lambda e: e.dma_start(out=PP[:, :], in_=pp_d), SEM_SETUP + 1, (), ["PP"])
        act(IDB[:, :], IDENT, AF.Copy, ["CST"], ["IDB"])
        P.op("dve", lambda e: e.memset(ONESB[:, :], 1.0), (), ["ONESB"])
        P.op("dve", lambda e: e.memset(SF[:, :, :], 0.0), (), ["SF"])
        P.op("dve", lambda e: e.memset(CAR[:, :], 0.0), (), [("CAR", i) for i in range(16)])
        P.op("dve", lambda e: e.memset(HIST[:, :, :], 0.0), (), [("HIST", i) for i in range(16)])
        P.op("dve", lambda e: e.memset(WK[:, :, :], 0.0), (), [("WK", i) for i in range(15)])
        P.op("dve", lambda e: e.memset(XS[:, :, :], 0.0), (), [("XS", 0), ("XS", 1)])
        PPv = PP[:, :].rearrange("p (l c) -> p l c", c=PL)
        DVv = DV[:, :].rearrange("p (l c) -> p l c", c=DL)
        T0 = TMPS[:, 0, :].rearrange("p (l c) -> p l c", c=4)
        T1 = TMPS[:, 1, :].rearrange("p (l c) -> p l c", c=4)
        act(DVv[:, :, HBA:HBA + 4], PPv[:, :, BA:BA + 4], AF.Copy, ["PP"], ["DV"], scale=0.5)
        act(DVv[:, :, HBX:HBX + 4], PPv[:, :, BX:BX + 4], AF.Copy, ["PP"], ["DV"], scale=0.5)
        act(DVv[:, :, HRG:HRG + 4], PPv[:, :, RG:RG + 4], AF.Copy, ["PP"], ["DV"], scale=0.5)
        act(T0, PPv[:, :, LAM:LAM + 4], AF.Exp, ["PP"], ["T0"], scale=-1.0)
        nterm = 8
        cf = [0.0] + [((-1.0) ** (k + 1)) / k for k in range(1, nterm + 1)]
        ts(T1, T0, cf[nterm], cf[nterm - 1], ALU.mult, ALU.add, ["T0"], ["T1"])
        for k in range(nterm - 2, 0, -1):
            tt(T1, T1, T0, ALU.mult, ["T1", "T0"], ["T1"])
            ts(T1, T1, cf[k], None, ALU.add, None, ["T1"], ["T1"])
        tt(T1, T1, T0, ALU.mult, ["T1", "T0"], ["T1"])
        ts(DVv[:, :, P4:P4 + 4], T1, 4.0, None, ALU.mult, None, ["T1"], ["DV"])
        ts(DVv[:, :, N4:N4 + 4], T1, -4.0, None, ALU.mult, None, ["T1"], ["DV"])
        ts(DVv[:, :, N8:N8 + 4], T1, -8.0, None, ALU.mult, None, ["T1"], ["DV"])

        def pp(l, off):
            return PP[:, l * PL + off: l * PL + off + 1]

        def dv(l, off):
            return DV[:, l * DL + off: l * DL + off + 1]

        def rms_feature(src_fn, nk, gain_fn, out_fn, blocks, srcreg, outreg, inv_n):
            for (c0, wd, vf) in blocks:
                sq = BK[:, 0:nk, 0:wd]
                for k in range(nk):
                    act(BK[:, k, 0:wd], src_fn(k, c0, wd), AF.Square, srcreg(k), [("BK", k)])
                b = ps1()
                for k in range(nk):
                    mm(PS[:, b, 0:wd], ONESB[:, :], BK[:, k, 0:wd], k == 0, k == nk - 1,
                       ["ONESB", ("BK", k)], [("PS", b)])
                act(WK[:, 12, 0:wd], PS[:, b, 0:wd], AF.Ln, [("PS", b)], [("WK", 12)], scale=inv_n, bias=EPSB[:, 0:1])
                act(WK[:, 12, 0:wd], WK[:, 12, 0:wd], AF.Exp, [("WK", 12)], [("WK", 12)], scale=-0.5)
                for k in range(nk):
                    stt(out_fn(k, c0, wd), src_fn(k, c0, wd), gain_fn(k), WK[:, 12, 0:wd], ALU.mult, ALU.mult,
                        srcreg(k) + [("WK", 12), "PP"], outreg(k))

        EPSB = sb("EPSB", [128, 2])
        P.op("dve", lambda e: e.memset(EPSB[:, 0:1], EPS), (), ["EPSB"])
        P.op("dve", lambda e: e.memset(EPSB[:, 1:2], 1.0), (), ["EPSB"])

        def norm_p1(blk):
            (c0, wd, vf) = blk
            for k in range(8):
                act(BK[:, 16 + k, 0:wd], H[:, k, c0:c0 + wd], AF.Square, [("H", k)], [("BK", 16 + k)])

        def norm_p2(l, off, blk):
            (c0, wd, vf) = blk
            b = ps1()
            for k in range(8):
                mm(PS[:, b, 0:wd], ONESB[:, :], BK[:, 16 + k, 0:wd], k == 0, k == 7,
                   ["ONESB", ("BK", 16 + k)], [("PS", b)])
            act(WK[:, 12, 0:wd], PS[:, b, 0:wd], AF.Ln, [("PS", b)], [("WK", 12)], scale=1.0 / D, bias=EPSB[:, 0:1])
            act(WK[:, 12, 0:wd], WK[:, 12, 0:wd], AF.Exp, [("WK", 12)], [("WK", 12)], scale=-0.5)
            for k in range(8):
                stt(HN[:, k, c0:c0 + wd], H[:, k, c0:c0 + wd], pp(l, off + k), WK[:, 12, 0:wd], ALU.mult, ALU.mult,
                    [("H", k), ("WK", 12), "PP"], [("HN", k, c0)])

        def norm_h(l, off, blocks):
            for blk in blocks:
                norm_p1(blk)
                norm_p2(l, off, blk)

        def lru_chunk(l, blk, c, st, sx, sg):
            (c0, wd, vf) = blk
            li = l * 4 + c
            W = lambda i: WK[:, st * 6 + i, 0:wd]
            wr = lambda i: ("WK", st * 6 + i)
            XSs = XS[:, st, :]
            xr = ("XS", st)
            xcb = BK[:, 8 + st, 0:wd]
            xbr = ("BK", 8 + st)
            hr = ("HIST", li)
            cr = ("CAR", li)
            bx, bg = yield from gps(2)
            for k in range(8):
                mm(PS[:, bx, 0:wd], RING[:, sx + c, k * 128:(k + 1) * 128], HN[:, k, c0:c0 + wd],
                   k == 0, k == 7, [("ring", sx + c), ("HN", k, c0)], [("PS", bx)])
            for k in range(8):
                mm(PS[:, bg, 0:wd], RING[:, sg + c, k * 128:(k + 1) * 128], HN[:, k, c0:c0 + wd],
                   k == 0, k == 7, [("ring", sg + c), ("HN", k, c0)], [("PS", bg)])
            yield
            act(XSs[:, 0:3], HIST[:, li, 0:3], AF.Copy, [hr], [xr])
            act(XSs[:, 3:3 + wd], PS[:, bx, 0:wd], AF.Copy, [("PS", bx)], [xr])
            ps_free(bx)
            act(HIST[:, li, 0:3], XSs[:, wd:wd + 3], AF.Copy, [xr], [hr])
            g0 = W(4)
            act(g0, PS[:, bg, 0:wd], AF.Square, [("PS", bg)], [wr(4)])
            yield
            xc = W(0)
            act(xc, XSs[:, 3:3 + wd], AF.Identity, [xr, "PP"], [wr(0)],
                scale=pp(l, CW + c * 4 + 3), bias=pp(l, CB + c))
            ts(g0, g0, 0.044715, 1.0, ALU.mult, ALU.add, [wr(4)], [wr(4)])
            yield
            for tap in (2, 1, 0):
                stt(xc, XSs[:, tap:tap + wd], pp(l, CW + c * 4 + tap), xc, ALU.mult, ALU.add,
                    [xr, wr(0), "PP"], [wr(0)])
                if tap == 2:
                    tt(g0, g0, PS[:, bg, 0:wd], ALU.mult, [wr(4), ("PS", bg)], [wr(4)])
                elif tap == 1:
                    act(g0, g0, AF.Tanh, [wr(4)], [wr(4)], scale=0.7978845608028654)
                else:
                    stt(g0, g0, 1.0, PS[:, bg, 0:wd], ALU.add, ALU.mult, [wr(4), ("PS", bg)], [wr(4)])
                    ps_free(bg)
                yield
            act(xcb, xc, AF.Copy, [wr(0)], [xbr])
            yield
            ba_, bb_ = yield from gps(2)
            gi = (c * 2) * 128
            mm(PS[:, ba_, 0:wd], GATES[:, gi:gi + 128], xcb, True, True, ["GATES", xbr], [("PS", ba_)])
            mm(PS[:, bb_, 0:wd], GATES[:, gi + 128:gi + 256], xcb, True, True, ["GATES", xbr], [("PS", bb_)])
            yield
            t1, t2, t3, t5 = W(1), W(2), W(3), W(5)
            act(t1, PS[:, ba_, 0:wd], AF.Tanh, [("PS", ba_), "DV"], [wr(1)], scale=0.5, bias=dv(l, HBA + c))
            act(t5, PS[:, bb_, 0:wd], AF.Tanh, [("PS", bb_), "DV"], [wr(5)], scale=0.5, bias=dv(l, HBX + c))
            ps_free(ba_, bb_)
            yield
            act(t2, t1, AF.Exp, [wr(1), "DV"], [wr(2)], scale=dv(l, N4 + c), bias=dv(l, N4 + c))
            act(t3, t1, AF.Tanh, [wr(1), "DV"], [wr(3)], scale=dv(l, P4 + c), bias=dv(l, P4 + c))
            stt(t5, t5, 1.0, xc, ALU.add, ALU.mult, [wr(5), wr(0)], [wr(5)])
            yield
            act(t1, t1, AF.Exp, [wr(1), "DV"], [wr(1)], scale=dv(l, N8 + c), bias=dv(l, N8 + c))
            yield
            stt(t1, t1, 1.0, t3, ALU.add, ALU.mult, [wr(1), wr(3)], [wr(1)])
            yield
            act(t1, t1, AF.Ln, [wr(1)], [wr(1)])
            act(t1, t1, AF.Exp, [wr(1)], [wr(1)], scale=0.5)
            yield
            stt(t5, t5, 0.5, t1, ALU.mult, ALU.mult, [wr(5), wr(1)], [wr(5)])
            yield
            if vf:
                P.op("dve", lambda e: e.memset(t1, 0.0), (), [wr(1)])
            P.op("dve", lambda e: e.tensor_tensor_scan(
                out=t1[:, vf:wd], data0=t2[:, vf:wd], data1=t5[:, vf:wd], initial=CAR[:, li:li + 1],
                op0=ALU.mult, op1=ALU.add), [wr(2), wr(5), cr], [wr(1)])
            yield
            act(CAR[:, li:li + 1], t1[:, wd - 1:wd], AF.Copy, [wr(1)], [cr])
            stt(BK[:, 24 + c, 0:wd], t1, 0.5, g0, ALU.mult, ALU.mult, [wr(1), wr(4)], [("BK", 24 + c)])
            yield
            if c == 3:
                for k in range(4):
                    act(BK[:, 14 + k, 0:wd], BK[:, 24 + k, 0:wd], AF.Square, [("BK", 24 + k)], [("BK", 14 + k)])
                yield
                b = yield from gps(1)
                for k in range(4):
                    mm(PS[:, b, 0:wd], ONESB[:, :], BK[:, 14 + k, 0:wd], k == 0, k == 3,
                       ["ONESB", ("BK", 14 + k)], [("PS", b)])
                yield
                act(WK[:, 12, 0:wd], PS[:, b, 0:wd], AF.Ln, [("PS", b)], [("WK", 12)], scale=1.0 / 512,
                    bias=EPSB[:, 0:1])
                ps_free(b)
                act(WK[:, 12, 0:wd], WK[:, 12, 0:wd], AF.Exp, [("WK", 12)], [("WK", 12)], scale=-0.5)
                yield
                for k in range(4):
                    stt(YM[:, k, c0:c0 + wd], BK[:, 24 + k, 0:wd], pp(l, OG + k), WK[:, 12, 0:wd], ALU.mult, ALU.mult,
                        [("BK", 24 + k), ("WK", 12), "PP"], [("YM", k)])
                    yield

        def ret_gate(l, blk, sgr):
            (c0, wd, vf) = blk
            yield "go"
            for hh in range(4):
                b = yield from gps(1)
                for k in range(8):
                    mm(PS[:, b, 0:wd], RING[:, sgr + hh, k * 128:(k + 1) * 128], HN[:, k, c0:c0 + wd],
                       k == 0, k == 7, [("ring", sgr + hh), ("HN", k, c0)], [("PS", b)])
                yield
                act(BK[:, 10 + hh, 0:wd], PS[:, b, 0:wd], AF.Tanh, [("PS", b)], [("BK", 10 + hh)], scale=0.5)
                yield
                stt(BK[:, 10 + hh, 0:wd], BK[:, 10 + hh, 0:wd], 1.0, PS[:, b, 0:wd], ALU.add, ALU.mult,
                    [("BK", 10 + hh), ("PS", b)], [("BK", 10 + hh)])
                ps_free(b)
                yield

        def ret_tile(l, lc, t0, gt, sq_, sk_, sv_, rs, tn, bc0):
            bi = lambda i: i if (rs == 0 or i < 2) else 16 + i
            B = lambda i: BK[:, bi(i), :]
            br = lambda i: ("BK", bi(i))
            SBr = SB2[:, (tn + 1) % 2, :]
            SBw = SB2[:, tn % 2, :]
            sbr_, sbw_ = ("SB2", (tn + 1) % 2), ("SB2", tn % 2)
            bqk = yield from gps(pair=True)
            for which, s0 in ((0, sq_), (1, sk_)):
                for k in range(8):
                    mm(PS[:, bqk + which, :], HN[:, k, lc:lc + 128], RING[:, s0:s0 + 4, k * 128:(k + 1) * 128],
                       k == 0, k == 7, [("HN", k, bc0)] + [("ring", s0 + j) for j in range(4)], [("PS", bqk + which)])
            bv = yield from gps(1)
            for k in range(8):
                mm(PS[:, bv, :], HN[:, k, lc:lc + 128], RING[:, sv_:sv_ + 4, k * 128:(k + 1) * 128],
                   k == 0, k == 7, [("HN", k, bc0)] + [("ring", sv_ + j) for j in range(4)], [("PS", bv)])
            yield
            act(B(2), PS[:, bv, :], AF.Copy, [("PS", bv)], [br(2)])
            ps_free(bv)
            qk = PS[:, bqk:bqk + 2, :].rearrange("p b (h t j) -> p (b h) t j", h=4, t=2, j=64)
            x1 = qk[:, :, 0, :]
            x2 = qk[:, :, 1, :]
            cs = COS(gt).unsqueeze(1).to_broadcast([128, 8, 64])
            sn = SIN(gt).unsqueeze(1).to_broadcast([128, 8, 64])
            ta = WK[:, 13, :].rearrange("p (h j) -> p h j", j=64)
            tb = WK[:, 14, :].rearrange("p (h j) -> p h j", j=64)
            qr = BK[:, 0:2, :].rearrange("p b (h t j) -> p (b h) t j", h=4, t=2, j=64)
            rqk = [("PS", bqk), ("PS", bqk + 1), "CS"]
            tt(ta, x1, cs, ALU.mult, rqk, [("WK", 13)])
            yield
            tt(tb, x2, sn, ALU.mult, rqk, [("WK", 14)])
            yield
            tt(qr[:, :, 0, :], ta, tb, ALU.subtract, [("WK", 13), ("WK", 14)], [("BK", 0), ("BK", 1)])
            yield
            tt(ta, x1, sn, ALU.mult, rqk, [("WK", 13)])
            yield
            tt(tb, x2, cs, ALU.mult, rqk, [("WK", 14)])
            ps_free(bqk, bqk + 1)
            yield
            tt(qr[:, :, 1, :], ta, tb, ALU.add, [("WK", 13), ("WK", 14)], [("BK", 0), ("BK", 1)])
            yield
            tt(B(3), BK[:, 1, :], ZETA, ALU.mult, [("BK", 1), "CST"], [br(3)])
            pqk = yield from gpsb()
            for hh in range(4):
                tr(PSB[:, pqk, hh * 128:(hh + 1) * 128], BK[:, 0, hh * 128:(hh + 1) * 128], IDB[:, :],
                   [("BK", 0), "IDB"], [("PSB", pqk)])
            for hh in range(4):
                tr(PSB[:, pqk, 512 + hh * 128:512 + (hh + 1) * 128], BK[:, 1, hh * 128:(hh + 1) * 128], IDB[:, :],
                   [("BK", 1), "IDB"], [("PSB", pqk)])
            yield
            bS = yield from gps(1)
            for hh in range(4):
                sl = slice(hh * 128, (hh + 1) * 128)
                mm(PS[:, bS, sl], B(3)[:, sl], B(2)[:, sl], True, True, [br(3), br(2)], [("PS", bS)])
            tt(B(4), PSB[:, pqk, 0:512], XI, ALU.mult, [("PSB", pqk), "CST"], [br(4)])
            act(B(5), PSB[:, pqk, 512:1024], AF.Copy, [("PSB", pqk), br(4)], [br(5)])
            psb_free(pqk)
            yield
            for hh in range(4):
                sl = slice(hh * 128, (hh + 1) * 128)
                stt(SF[:, l, sl], SF[:, l, sl], GAM[hh] ** 128, PS[:, bS, sl], ALU.mult, ALU.add,
                    ["SF", ("PS", bS)], ["SF"])
            ps_free(bS)
            yield
            act(SBw, SF[:, l, :], AF.Copy, ["SF"], [sbw_])
            yield "go"
            bs = yield from gps(1)
            for hh in range(4):
                sl = slice(hh * 128, (hh + 1) * 128)
                mm(PS[:, bs, sl], B(5)[:, sl], B(4)[:, sl], True, True, [br(5), br(4)], [("PS", bs)])
            yield
            tt(B(6), PS[:, bs, :], MASK, ALU.mult, [("PS", bs), "CST"], [br(6)])
            ps_free(bs)
            yield
            by = yield from gps(1)
            for hh in range(4):
                sl = slice(hh * 128, (hh + 1) * 128)
                mm(PS[:, by, sl], B(6)[:, sl], B(2)[:, sl], True, False, [br(6), br(2)], [("PS", by)])
                mm(PS[:, by, sl], B(4)[:, sl], SBr[:, sl], False, True, [br(4), sbr_], [("PS", by)])
            yield
            for hh in range(4):
                sl = slice(hh * 128, (hh + 1) * 128)
                P.op("dve", lambda e, hh=hh, sl=sl: e.bn_stats(out=STAT[:, rs, hh, 0:6], in_=PS[:, by, sl]),
                     [("PS", by)], [("STAT", rs, hh)])
                yield
            for hh in range(4):
                P.op("dve", lambda e, hh=hh: e.bn_aggr(out=MV[:, rs, hh, :], in_=STAT[:, rs, hh, 0:6]),
                     [("STAT", rs, hh)], [("MV", rs)])
            yield
            act(RSTD[:, rs * 4:rs * 4 + 4], MV[:, rs, :, 1], AF.Ln, [("MV", rs)], [("RSTD", rs)], bias=EPSB[:, 0:1])
            act(RSTD[:, rs * 4:rs * 4 + 4], RSTD[:, rs * 4:rs * 4 + 4], AF.Exp, [("RSTD", rs)], [("RSTD", rs)],
                scale=-0.5)
            yield
            for hh in range(4):
                sl = slice(hh * 128, (hh + 1) * 128)
                ts(B(7)[:, sl], PS[:, by, sl], MV[:, rs, hh, 0:1], RSTD[:, rs * 4 + hh:rs * 4 + hh + 1],
                   ALU.subtract, ALU.mult, [("PS", by), ("MV", rs), ("RSTD", rs)], [br(7)])
            ps_free(by)
            yield
            py = yield from gpsb()
            for hh in range(4):
                tr(PSB[:, py, hh * 128:(hh + 1) * 128], B(7)[:, hh * 128:(hh + 1) * 128], IDB[:, :],
                   [br(7), "IDB"], [("PSB", py)])
            yield
            for hh in range(4):
                stt(YM[:, 4 + hh, lc:lc + 128], PSB[:, py, hh * 128:(hh + 1) * 128], dv(l, HRG + hh),
                    BK[:, 10 + hh, t0:t0 + 128], ALU.mult, ALU.mult,
                    [("PSB", py), "DV", ("BK", 10 + hh)], [("YM", 4 + hh)])
                if hh == 3:
                    psb_free(py)
                yield

        class Stream:
            def __init__(self, groups, width, gated=False):
                self.groups = groups
                self.gi = 0
                self.cur = []
                self.active = []
                self.width = width
                self.gated = gated
                self.open = True

            def step(self):
                if not self.active and not self.cur:
                    if self.gi >= len(self.groups):
                        return False
                    self.cur = list(self.groups[self.gi])
                    self.gi += 1
                    self.open = True
                while len(self.active) < self.width and self.cur and (self.open or not self.gated):
                    self.active.append(self.cur.pop(0))
                    self.open = False
                for g in list(self.active):
                    try:
                        r = next(g)
                        if r == "go":
                            self.open = True
                    except StopIteration:
                        self.active.remove(g)
                        if g is getattr(self, "last", None):
                            pass
                return True

        def mixer_phase(l, blocks, hc0):
            rp[0] = 0
            sx = ring_alloc(4)
            for c in range(4):
                load_cols(win_d, l, c * 128, sx + c)
            sg = ring_alloc(4)
            for c in range(4):
                load_cols(win_d, l, 512 + c * 128, sg + c)
            P.dma("pool", lambda e: e.dma_start(out=GATES[:, :], in_=gates_d[:, l * 1024:(l + 1) * 1024]),
                  SEM_SETUP + 3, (), ["GATES"])
            sq_ = ring_alloc(4, 4)
            sk_ = ring_alloc(4, 4)
            sv_ = ring_alloc(4, 4)
            sgr = ring_alloc(4, 4)
            for base, s0 in ((1024, sq_), (1536, sk_), (2048, sv_), (2560, sgr)):
                for c in range(4):
                    load_cols(win_d, l, base + c * 128, s0 + c)
            act(SB2[:, 1, :], SF[:, l, :], AF.Copy, ["SF"], [("SB2", 1)])
            lru_groups = []
            ui = 0
            for blk in blocks:
                g = []
                for c in range(4):
                    g.append(lru_chunk(l, blk, c, ui % 2, sx, sg))
                    ui += 1
                lru_groups.append(g)
            ret_groups = []
            tn = 0
            for blk in blocks:
                (c0, wd, vf) = blk
                g = [ret_gate(l, blk, sgr)]
                for t0 in range(0, wd, 128):
                    lc = c0 + t0
                    g.append(ret_tile(l, lc, t0, (hc0 + lc) // 128, sq_, sk_, sv_, tn % 2, tn, c0))
                    tn += 1
                ret_groups.append(g)
            streams = [Stream(ret_groups, 2, gated=True), Stream(lru_groups, 2)]
            alive = True
            while alive:
                alive = False
                for st in streams:
                    if st.step():
                        alive = True

        def resid_add(b, dc, c0, wd, vf):
            tt(H[:, dc, c0 + vf:c0 + wd], PS[:, b, vf:wd], H[:, dc, c0 + vf:c0 + wd], ALU.add,
               [("PS", b), ("H", dc)], [("H", dc)])

        def out_phase(l, blocks):
            so = ring_alloc(8)
            for dc in range(8):
                load_cols(wout_d, l, dc * 128, so + dc)
            for i, (c0, wd, vf) in enumerate(blocks):
                for dc in range(8):
                    b = ps1()
                    for k in range(8):
                        mm(PS[:, b, 0:wd], RING[:, so + dc, k * 128:(k + 1) * 128], YM[:, k, c0:c0 + wd],
                           k == 0, k == 7, [("ring", so + dc), ("YM", k)], [("PS", b)])
                    resid_add(b, dc, c0, wd, vf)
                if i > 0:
                    norm_p2(l, NF, blocks[i - 1])
                norm_p1(blocks[i])
            norm_p2(l, NF, blocks[-1])

        def ffn_phase(l, blocks, next_norm):
            j0 = 0
            for gidx, ng in enumerate(FFG):
                sgs, sus, sds = [], [], []
                for j in range(j0, j0 + ng):
                    s = ring_alloc(3)
                    load_cols(wg_d, l, j * 128, s)
                    load_cols(wu_d, l, j * 128, s + 1)
                    load_rows(wd_d, l, j * 128, s + 2)
                    sgs.append(s)
                    sus.append(s + 1)
                    sds.append(s + 2)
                for bi, (c0, wd, vf) in enumerate(blocks):
                    for jj in range(ng):
                        bg = ps1()
                        bu = ps1()
                        for k in range(8):
                            mm(PS[:, bg, 0:wd], RING[:, sgs[jj], k * 128:(k + 1) * 128], HN[:, k, c0:c0 + wd],
                               k == 0, k == 7, [("ring", sgs[jj]), ("HN", k, c0)], [("PS", bg)])
                        for k in range(8):
                            mm(PS[:, bu, 0:wd], RING[:, sus[jj], k * 128:(k + 1) * 128], HN[:, k, c0:c0 + wd],
                               k == 0, k == 7, [("ring", sus[jj]), ("HN", k, c0)], [("PS", bu)])
                        wi = jj % 2
                        t = WK[:, wi, 0:wd]
                        act(t, PS[:, bg, 0:wd], AF.Silu, [("PS", bg)], [("WK", wi)])
                        ai = (bi % 2) * 4 + jj
                        tt(BK[:, ai, 0:wd], t, PS[:, bu, 0:wd], ALU.mult, [("WK", wi), ("PS", bu)], [("BK", ai)])
                    for dc in range(8):
                        b = ps1()
                        for jj in range(ng):
                            ai = (bi % 2) * 4 + jj
                            mm(PS[:, b, 0:wd], RING[:, sds[jj], dc * 128:(dc + 1) * 128], BK[:, ai, 0:wd],
                               jj == 0, jj == ng - 1, [("ring", sds[jj]), ("BK", ai)], [("PS", b)])
                        resid_add(b, dc, c0, wd, vf)
                    if next_norm and gidx == len(FFG) - 1:
                        if bi > 0:
                            norm_p2(l + 1, NM, blocks[bi - 1])
                        norm_p1(blocks[bi])
                if next_norm and gidx == len(FFG) - 1:
                    norm_p2(l + 1, NM, blocks[-1])
                j0 += ng

        for hi, (t_lo, t_hi) in enumerate(HALVES):
            hc0 = t_lo * 128
            ntile = t_hi - t_lo
            if hi == 0:
                blocks = [(0, 384, PADF), (384, 384, 0), (768, 384, 0)]
            else:
                blocks = [(0, 512, 0), (512, 512, 0)]
            tlo[0] = t_lo
            P.dma("sp", lambda e, hi=hi: e.dma_start(out=CS[:, :, :].rearrange("p a f -> p (a f)"),
                                                      in_=cs_d[:, hi * 1152:(hi + 1) * 1152]),
                  SEM_SETUP + 2, (), ["CS"])
            for ti in range(ntile):
                gt = t_lo + ti
                wb = (ti % 2) * 2
                xst = WK[:, wb:wb + 2, :].rearrange("p a f -> p (a f)")
                regs = [("WK", wb), ("WK", wb + 1)]
                if gt == 0:
                    P.op("dve", lambda e, xst=xst: e.memset(xst, 0.0), (), regs)
                    P.dma("sp", lambda e, xst=xst: e.dma_start(out=xst[PADF:128, :], in_=meta_d), SEM_XST[ti % 2],
                          (), regs)
                else:
                    r0 = (gt - 1) * 128
                    P.dma("sp", lambda e, xst=xst, r0=r0: e.dma_start(out=xst, in_=x_d[r0:r0 + 128, :]),
                          SEM_XST[ti % 2], (), regs)
                b = ps2()
                for k in range(8):
                    bb = b + k // 4
                    tr(PS[:, bb, (k % 4) * 128:(k % 4 + 1) * 128], xst[:, k * 128:(k + 1) * 128], IDENT,
                       regs + ["CST"], [("PS", bb)])
                for k in range(8):
                    bb = b + k // 4
                    act(H[:, k, ti * 128:(ti + 1) * 128], PS[:, bb, (k % 4) * 128:(k % 4 + 1) * 128], AF.Copy,
                        [("PS", bb)], [("H", k)])
            for l in range(DEPTH):
                if l == 0:
                    norm_h(l, NM, blocks)
                mixer_phase(l, blocks, hc0)
                if DBG is not None and DBG == (l, hi):
                    P.dma("sp", lambda e: e.dma_start(out=dbg_d, in_=YM[:, :, :].rearrange("p a f -> p (a f)")),
                          SEM_SETUP + 1, [("YM", k) for k in range(8)], ())
                out_phase(l, blocks)
                ffn_phase(l, blocks, l < DEPTH - 1)
            P.dma("sp", lambda e: e.dma_start(out=GFIN, in_=gfin_d), SEM_SETUP + 3 + 1, (), [("WK", 8), ("WK", 9)])
            for ti in range(ntile):
                gt = t_lo + ti
                if gt == 0:
                    continue
                b = ps2()
                for k in range(8):
                    bb = b + k // 4
                    tr(PS[:, bb, (k % 4) * 128:(k % 4 + 1) * 128], H[:, k, ti * 128:(ti + 1) * 128], IDENT,
                       [("H", k), "CST"], [("PS", bb)])
                wb = (ti % 2) * 2
                ho = WK[:, wb:wb + 2, :].rearrange("p a f -> p (a f)")
                regs = [("WK", wb), ("WK", wb + 1)]
                for a in range(2):
                    act(ho[:, a * 512:(a + 1) * 512], PS[:, b + a, :], AF.Copy, [("PS", b + a)], regs)
                ob = 4 + (ti % 2) * 2
                ot = WK[:, ob:ob + 2, :].rearrange("p a f -> p (a f)")
                oregs = [("WK", ob), ("WK", ob + 1)]
                act(ot, ho, AF.Square, regs, oregs + ["RSTD"], accum=RSTD[:, 8:9])
                act(RSTD[:, 8:9], RSTD[:, 8:9], AF.Ln, ["RSTD"], ["RSTD"], scale=1.0 / D, bias=EPSB[:, 0:1])
                act(RSTD[:, 8:9], RSTD[:, 8:9], AF.Exp, ["RSTD"], ["RSTD"], scale=-0.5)
                stt(ot, ho, RSTD[:, 8:9], GFIN, ALU.mult, ALU.mult, regs + ["RSTD", ("WK", 8), ("WK", 9)], oregs)
                r0 = (gt - 1) * 128
                P.dma("sp", lambda e, ot=ot, r0=r0: e.dma_start(out=out_d[r0:r0 + 128, :], in_=ot),
                      SEM_OUT[ti % 2], oregs, ())
        P.op("sp", None, (), [("WK", i) for i in range(4, 8)])

        P.finalize()
        with nc.Block() as block:
            @block.tensor
            def _(e):
                P.emit("pe", e, esems, dsems)

            @block.scalar
            def _(e):
                P.emit("act", e, esems, dsems)

            @block.vector
            def _(e):
                P.emit("dve", e, esems, dsems)

            @block.gpsimd
            def _(e):
                P.emit("pool", e, esems, dsems)

            @block.sync
            def _(e):
                P.emit("sp", e, esems, dsems)
    return nc


def _host_tables():
    pos = np.maximum(np.arange(TP, dtype=np.float32) - PADF, 0.0).astype(np.float32)
    half = 64
    inv = (np.float32(10000.0) ** (-np.arange(half, dtype=np.float32) / np.float32(half))).astype(np.float32)
    ang = (pos[:, None] * inv[None, :]).astype(np.float32)
    cos = np.cos(ang).astype(np.float32).reshape(NT, 128, 64).transpose(1, 0, 2)
    sin = np.sin(ang).astype(np.float32).reshape(NT, 128, 64).transpose(1, 0, 2)
    cs = np.zeros((128, 2, 2, 9, 64), np.float32)
    for hi, (a, b) in enumerate(HALVES):
        cs[:, hi, 0, 0:b - a] = cos[:, a:b]
        cs[:, hi, 1, 0:b - a] = sin[:, a:b]
    idx = np.arange(128, dtype=np.float64)
    mask = np.zeros((128, 4, 128), np.float32)
    xi = np.zeros((128, 4, 128), np.float32)
    zeta = np.zeros((128, 4, 128), np.float32)
    for h in range(4):
        g = GAM[h]
        m = np.where(idx[None, :] >= idx[:, None], (g ** (-(idx + 1.0)))[:, None], 0.0)
        mask[:, h, :] = m
        xi[:, h, :] = ((g ** (idx + 1.0)) * (128.0 ** -0.5))[None, :]
        zeta[:, h, :] = (g ** (127.0 - idx))[:, None]
    ident = np.eye(128, dtype=np.float32)
    cst = np.concatenate([ident, mask.reshape(128, 512), xi.reshape(128, 512), zeta.reshape(128, 512)], axis=1)
    return np.ascontiguousarray(cst.astype(np.float32)), np.ascontiguousarray(cs.reshape(128, -1))


_NC_CACHE = {}


def kernel(x, meta_tokens, norm_mix, w_in, conv_w, conv_b, gate_a_w, gate_a_b, gate_x_w, gate_x_b,
           lru_lambda, lru_out_norm, ret_out_norm, w_out, norm_ffn, w_gate, w_up, w_down, norm_final):
    f = lambda a: np.ascontiguousarray(np.asarray(a, dtype=np.float32))
    x = f(x)
    pp = np.zeros((128, DEPTH * PL), np.float32)
    gates = np.zeros((128, DEPTH * 4 * 2 * 128), np.float32)
    gaw, gxw = f(gate_a_w), f(gate_x_w)
    for l in range(DEPTH):
        o = l * PL
        pp[:, o + NM:o + NM + 8] = f(norm_mix)[l].reshape(8, 128).T
        pp[:, o + NF:o + NF + 8] = f(norm_ffn)[l].reshape(8, 128).T
        cw = f(conv_w)[l]
        for c in range(4):
            pp[:, o + CW + c * 4:o + CW + c * 4 + 4] = cw[:, c * 128:(c + 1) * 128].T
        pp[:, o + CB:o + CB + 4] = f(conv_b)[l].reshape(4, 128).T
        pp[:, o + BA:o + BA + 4] = f(gate_a_b)[l].reshape(4, 128).T
        pp[:, o + BX:o + BX + 4] = f(gate_x_b)[l].reshape(4, 128).T
        pp[:, o + LAM:o + LAM + 4] = f(lru_lambda)[l].reshape(4, 128).T
        pp[:, o + OG:o + OG + 4] = f(lru_out_norm)[l].reshape(4, 128).T
        pp[:, o + RG:o + RG + 4] = f(ret_out_norm)[l].reshape(4, 128).T
        for c in range(4):
            for wi, gw in enumerate((gaw, gxw)):
                base = ((l * 4 + c) * 2 + wi) * 128
                gates[0:64, base:base + 64] = gw[l, 2 * c]
                gates[64:128, base + 64:base + 128] = gw[l, 2 * c + 1]
    cst, cs = _host_tables()
    gfin = np.ascontiguousarray(np.broadcast_to(f(norm_final)[None, :], (128, D)))
    if "nc" not in _NC_CACHE:
        _NC_CACHE["nc"] = build_nc()
    nc = _NC_CACHE["nc"]
    shared = {"meta": f(meta_tokens), "w_in": f(w_in), "w_out": f(w_out), "w_gate": f(w_gate), "w_up": f(w_up),
              "w_down": f(w_down), "gates": gates, "pp": pp, "cst": cst, "cs": cs, "gfin": gfin}
    in_maps = [dict(shared, x=x[b]) for b in range(8)]
    res = run_bass_kernel_spmd(nc, in_maps, core_ids=list(range(8)))
    if DBG is not None:
        DBG_OUT["dbg"] = np.asarray(res.results[0]["dbg"])
    return np.stack([np.asarray(r["out"], dtype=np.float32) for r in res.results], axis=0)
```

```python
import contextlib
import numpy as np
import concourse.bass as bass
import concourse.mybir as mybir
from concourse.bass_utils import run_bass_kernel_spmd

F32 = mybir.dt.float32
BF16 = mybir.dt.bfloat16
AF = mybir.ActivationFunctionType
ALU = mybir.AluOpType

D = 1024
SEQ = 2048
NMETA = 16
DEPTH = 4
DFF = 2816
NFF = DFF // 128
PADF = 112
TP = 2176
NT = TP // 128
HALVES = [(0, 9), (9, 17)]
HW = 9 * 128
RS = 24
EPS = 1e-6
SELF_SYNC = True
DBG = None
DBG_OUT = {}

NM, NF, CW, CB, BA, BX, LAM, OG, RG, PL = 0, 8, 16, 32, 36, 40, 44, 48, 52, 56
HBA, HBX, P4, N4, N8, HRG, DL = 0, 4, 8, 12, 16, 20, 24
GAM = [1.0 - 2.0 ** (-5.0 - h) for h in range(4)]
FFG = [4, 4, 4, 4, 3, 3]


class Prog:
    def __init__(self):
        self.ops = {k: [] for k in ("pe", "act", "dve", "pool", "sp")}
        self.last_w = {}
        self.readers = {}
        self.dma_cnt = {}
        self.touch = {}
        self.clock = 0

    def _deps(self, reads, writes, tok):
        deps = []
        self.clock += 1
        for r in list(reads) + list(writes):
            self.touch[r] = self.clock
        for r in reads:
            t = self.last_w.get(r)
            if t is not None:
                deps.append(t)
        for w in writes:
            t = self.last_w.get(w)
            if t is not None:
                deps.append(t)
            deps.extend(self.readers.get(w, ()))
        for r in reads:
            self.readers.setdefault(r, []).append(tok)
        for w in writes:
            self.last_w[w] = tok
            self.readers[w] = []
        return deps

    def op(self, eng, fn, reads=(), writes=()):
        lst = self.ops[eng]
        tok = ("E", eng, len(lst))
        deps = self._deps(reads, writes, tok)
        lst.append({"fn": fn, "deps": deps, "tok": tok, "inc": False})

    def dma(self, q, fn, sem, reads=(), writes=()):
        self.dma_cnt[sem] = self.dma_cnt.get(sem, 0) + 16
        tok = ("D", sem, self.dma_cnt[sem])
        deps = self._deps(reads, writes, tok)
        self.ops[q].append({"fn": fn, "deps": deps, "tok": tok, "inc": False, "dsem": sem})

    def finalize(self):
        for eng, lst in self.ops.items():
            for o in lst:
                nd = []
                for d in set(o["deps"]):
                    if d[0] == "E":
                        if d[1] == eng and (eng == "pe" or not SELF_SYNC):
                            continue
                        if d == o["tok"]:
                            continue
                        self.ops[d[1]][d[2]]["inc"] = True
                    nd.append(d)
                o["deps"] = nd
        self.cnt = {}
        for eng, lst in self.ops.items():
            c = 0
            arr = []
            for o in lst:
                if o["inc"]:
                    c += 1
                arr.append(c)
            self.cnt[eng] = arr

    def emit(self, eng, e, esems, dsems):
        known = {}
        for o in self.ops[eng]:
            need = {}
            for d in o["deps"]:
                if d[0] == "E":
                    key = ("E", d[1])
                    val = self.cnt[d[1]][d[2]]
                else:
                    key = ("D", d[1])
                    val = d[2]
                if val > known.get(key, 0):
                    need[key] = max(need.get(key, 0), val)
            for key, val in need.items():
                sem = esems[key[1]] if key[0] == "E" else dsems[key[1]]
                e.wait_ge(sem, val)
                known[key] = val
            if o["fn"] is None:
                continue
            ins = o["fn"](e)
            if "dsem" in o:
                ins.then_inc(dsems[o["dsem"]], 16)
            elif o["inc"]:
                ins.then_inc(esems[eng], 1)


def build_nc():
    nc = bass.Bass("TRN2", target_bir_lowering=False)
    dt = lambda n, s: nc.dram_tensor(n, s, F32, kind="ExternalInput").ap()
    x_d = dt("x", [SEQ, D])
    meta_d = dt("meta", [NMETA, D])
    win_d = dt("w_in", [DEPTH, D, 3072])
    wout_d = dt("w_out", [DEPTH, D, D])
    wg_d = dt("w_gate", [DEPTH, D, DFF])
    wu_d = dt("w_up", [DEPTH, D, DFF])
    wd_d = dt("w_down", [DEPTH, DFF, D])
    gates_d = dt("gates", [128, DEPTH * 4 * 2 * 128])
    pp_d = dt("pp", [128, DEPTH * PL])
    cst_d = dt("cst", [128, 128 + 3 * 512])
    cs_d = dt("cs", [128, 2 * 2 * 576])
    gfin_d = dt("gfin", [128, D])
    out_d = nc.dram_tensor("out", [SEQ, D], F32, kind="ExternalOutput").ap()
    if DBG is not None:
        dbg_d = nc.dram_tensor("dbg", [128, 8 * HW], BF16, kind="ExternalOutput").ap()

    P = Prog()
    es = contextlib.ExitStack()
    with es:
        sb = lambda n, s, d=F32: es.enter_context(nc.sbuf_tensor(n, s, d))
        H = sb("H", [128, 8, HW])
        HN = sb("HN", [128, 8, HW], BF16)
        YM = sb("YM", [128, 8, HW], BF16)
        RING = sb("RING", [128, RS, 1024], BF16)
        CST = sb("CST", [128, 128 + 3 * 512])
        CS = sb("CS", [128, 2, 576])
        GATES = sb("GATES", [128, 1024], BF16)
        PP = sb("PP", [128, DEPTH * PL])
        DV = sb("DV", [128, DEPTH * DL])
        TMPS = sb("TMPS", [128, 3, 16])
        IDB = sb("IDB", [128, 128], BF16)
        ONESB = sb("ONESB", [128, 128], BF16)
        SF = sb("SF", [128, DEPTH, 512])
        SB2 = sb("SB2", [128, 2, 512], BF16)
        CAR = sb("CAR", [128, DEPTH * 4])
        HIST = sb("HIST", [128, DEPTH * 4, 4])
        XS = sb("XS", [128, 2, 520])
        WK = sb("WK", [128, 15, 512])
        BK = sb("BK", [128, 28, 512], BF16)
        STAT = sb("STAT", [128, 2, 4, 8])
        MV = sb("MV", [128, 2, 4, 2])
        RSTD = sb("RSTD", [128, 12])
        PS = es.enter_context(nc.psum_tensor("PS", [128, 6, 512], F32))
        PSB = es.enter_context(nc.psum_tensor("PSB", [128, 2, 1024], BF16))
        esems = {k: es.enter_context(nc.semaphore("s_" + k)) for k in ("pe", "act", "dve")}
        ND = RS + 9
        dsems = [es.enter_context(nc.semaphore("d%d" % i)) for i in range(ND)]
        SEM_XST = [RS, RS + 1]
        SEM_OUT = [RS + 2, RS + 3]
        SEM_SETUP = RS + 4

        IDENT = CST[:, 0:128]
        MASK = CST[:, 128:640]
        XI = CST[:, 640:1152]
        ZETA = CST[:, 1152:1664]
        GFIN = WK[:, 8:10, :].rearrange("p a f -> p (a f)")
        tlo = [0]

        def COS(gt):
            return CS[:, 0, (gt - tlo[0]) * 64:(gt - tlo[0] + 1) * 64]

        def SIN(gt):
            return CS[:, 1, (gt - tlo[0]) * 64:(gt - tlo[0] + 1) * 64]

        def act(out, in_, func, r, w, scale=None, bias=None, accum=None):
            kw = {}
            if scale is not None:
                kw["scale"] = scale
            if bias is not None:
                kw["bias"] = bias
            if accum is not None:
                kw["accum_out"] = accum
            P.op("act", lambda e: e.activation(out=out, in_=in_, func=func, **kw), r, w)

        def tt(out, in0, in1, op, r, w, eng="dve"):
            P.op(eng, lambda e: e.tensor_tensor(out=out, in0=in0, in1=in1, op=op), r, w)

        def stt(out, in0, scalar, in1, op0, op1, r, w):
            P.op("dve", lambda e: e.scalar_tensor_tensor(out=out, in0=in0, scalar=scalar, in1=in1,
                                                         op0=op0, op1=op1), r, w)

        def ts(out, in0, s1, s2, op0, op1, r, w):
            if op1 is None:
                P.op("dve", lambda e: e.tensor_scalar(out=out, in0=in0, scalar1=s1, scalar2=None, op0=op0), r, w)
            else:
                P.op("dve", lambda e: e.tensor_scalar(out=out, in0=in0, scalar1=s1, scalar2=s2,
                                                      op0=op0, op1=op1), r, w)

        def mm(out, lhsT, rhs, start, stop, r, w):
            P.op("pe", lambda e: e.matmul(out, lhsT, rhs, start=start, stop=stop), r, w)

        def tr(out, in_, ident, r, w):
            P.op("pe", lambda e: e.transpose(out, in_, ident), r, w)

        ps_busy = [False] * 6

        def ps_try(n=1, pair=False):
            tch = lambda i: P.touch.get(("PS", i), -1)
            if pair:
                c = [i for i in (0, 2, 4) if not ps_busy[i] and not ps_busy[i + 1]]
                if not c:
                    return None
                b = min(c, key=lambda i: max(tch(i), tch(i + 1)))
                ps_busy[b] = ps_busy[b + 1] = True
                P.clock += 1
                P.touch[("PS", b)] = P.touch[("PS", b + 1)] = P.clock
                return b
            free = sorted([i for i in range(6) if not ps_busy[i]], key=tch)
            if len(free) < n:
                return None
            for b in free[:n]:
                ps_busy[b] = True
                P.clock += 1
                P.touch[("PS", b)] = P.clock
            return free[0] if n == 1 else free[:n]

        def ps_free(*bs):
            for b in bs:
                ps_busy[b] = False

        def gps(n=1, pair=False):
            while True:
                r = ps_try(n, pair)
                if r is not None:
                    return r
                yield

        def ps1():
            b = ps_try(1)
            assert b is not None
            ps_busy[b] = False
            return b

        def ps2():
            b = ps_try(pair=True)
            assert b is not None
            ps_busy[b] = ps_busy[b + 1] = False
            return b

        psb_busy = [False, False]

        def gpsb():
            while True:
                for i in (0, 1):
                    if not psb_busy[i]:
                        psb_busy[i] = True
                        return i
                yield

        def psb_free(i):
            psb_busy[i] = False

        psbi = [0]

        def psb1():
            b = psbi[0] % 2
            psbi[0] += 1
            return b

        rp = [0]

        def ring_alloc(n, align=1):
            if rp[0] % align:
                rp[0] += align - rp[0] % align
            if rp[0] + n > RS:
                rp[0] = 0
            s0 = rp[0]
            rp[0] += n
            return s0

        def load_cols(w3, l, c0, s):
            src = w3[l].rearrange("(kc p) f -> p kc f", p=128)[:, :, c0:c0 + 128]
            dst = RING[:, s, :].rearrange("p (kc f) -> p kc f", kc=8)
            P.dma("pool", lambda e: e.dma_start(out=dst, in_=src), s, (), [("ring", s)])

        def load_rows(w3, l, r0, s):
            src = w3[l][r0:r0 + 128, :]
            dst = RING[:, s, :]
            P.dma("pool", lambda e: e.dma_start(out=dst, in_=src), s, (), [("ring", s)])

        P.dma("sp", lambda e: e.dma_start(out=CST[:, :], in_=cst_d), SEM_SETUP, (), ["CST"])
        P.dma("sp", lambda e: e.dma_start(out=PP[:, :], in_=pp_d), SEM_SETUP + 1, (), ["PP"])
        act(IDB[:, :], IDENT, AF.Copy, ["CST"], ["IDB"])
        P.op("dve", lambda e: e.memset(ONESB[:, :], 1.0), (), ["ONESB"])
        P.op("dve", lambda e: e.memset(SF[:, :, :], 0.0), (), ["SF"])
        P.op("dve", lambda e: e.memset(CAR[:, :], 0.0), (), [("CAR", i) for i in range(16)])
        P.op("dve", lambda e: e.memset(HIST[:, :, :], 0.0), (), [("HIST", i) for i in range(16)])
        P.op("dve", lambda e: e.memset(WK[:, :, :], 0.0), (), [("WK", i) for i in range(15)])
        P.op("dve", lambda e: e.memset(XS[:, :, :], 0.0), (), [("XS", 0), ("XS", 1)])
        PPv = PP[:, :].rearrange("p (l c) -> p l c", c=PL)
        DVv = DV[:, :].rearrange("p (l c) -> p l c", c=DL)
        T0 = TMPS[:, 0, :].rearrange("p (l c) -> p l c", c=4)
        T1 = TMPS[:, 1, :].rearrange("p (l c) -> p l c", c=4)
        act(DVv[:, :, HBA:HBA + 4], PPv[:, :, BA:BA + 4], AF.Copy, ["PP"], ["DV"], scale=0.5)
        act(DVv[:, :, HBX:HBX + 4], PPv[:, :, BX:BX + 4], AF.Copy, ["PP"], ["DV"], scale=0.5)
        act(DVv[:, :, HRG:HRG + 4], PPv[:, :, RG:RG + 4], AF.Copy, ["PP"], ["DV"], scale=0.5)
        act(T0, PPv[:, :, LAM:LAM + 4], AF.Exp, ["PP"], ["T0"], scale=-1.0)
        nterm = 8
        cf = [0.0] + [((-1.0) ** (k + 1)) / k for k in range(1, nterm + 1)]
        ts(T1, T0, cf[nterm], cf[nterm - 1], ALU.mult, ALU.add, ["T0"], ["T1"])
        for k in range(nterm - 2, 0, -1):
            tt(T1, T1, T0, ALU.mult, ["T1", "T0"], ["T1"])
            ts(T1, T1, cf[k], None, ALU.add, None, ["T1"], ["T1"])
        tt(T1, T1, T0, ALU.mult, ["T1", "T0"], ["T1"])
        ts(DVv[:, :, P4:P4 + 4], T1, 4.0, None, ALU.mult, None, ["T1"], ["DV"])
        ts(DVv[:, :, N4:N4 + 4], T1, -4.0, None, ALU.mult, None, ["T1"], ["DV"])
        ts(DVv[:, :, N8:N8 + 4], T1, -8.0, None, ALU.mult, None, ["T1"], ["DV"])

        def pp(l, off):
            return PP[:, l * PL + off: l * PL + off + 1]

        def dv(l, off):
            return DV[:, l * DL + off: l * DL + off + 1]

        def rms_feature(src_fn, nk, gain_fn, out_fn, blocks, srcreg, outreg, inv_n):
            for (c0, wd, vf) in blocks:
                sq = BK[:, 0:nk, 0:wd]
                for k in range(nk):
                    act(BK[:, k, 0:wd], src_fn(k, c0, wd), AF.Square, srcreg(k), [("BK", k)])
                b = ps1()
                for k in range(nk):
                    mm(PS[:, b, 0:wd], ONESB[:, :], BK[:, k, 0:wd], k == 0, k == nk - 1,
                       ["ONESB", ("BK", k)], [("PS", b)])
                act(WK[:, 12, 0:wd], PS[:, b, 0:wd], AF.Ln, [("PS", b)], [("WK", 12)], scale=inv_n, bias=EPSB[:, 0:1])
                act(WK[:, 12, 0:wd], WK[:, 12, 0:wd], AF.Exp, [("WK", 12)], [("WK", 12)], scale=-0.5)
                for k in range(nk):
                    stt(out_fn(k, c0, wd), src_fn(k, c0, wd), gain_fn(k), WK[:, 12, 0:wd], ALU.mult, ALU.mult,
                        srcreg(k) + [("WK", 12), "PP"], outreg(k))

        EPSB = sb("EPSB", [128, 2])
        P.op("dve", lambda e: e.memset(EPSB[:, 0:1], EPS), (), ["EPSB"])
        P.op("dve", lambda e: e.memset(EPSB[:, 1:2], 1.0), (), ["EPSB"])

        SQI = [14, 15, 16, 17, 24, 25, 26, 27]

        def norm_p1(blk):
            (c0, wd, vf) = blk
            for k in range(8):
                act(BK[:, SQI[k], 0:wd], H[:, k, c0:c0 + wd], AF.Square, [("H", k)], [("BK", SQI[k])])

        def norm_p2(l, off, blk, b=None):
            (c0, wd, vf) = blk
            if b is None:
                b = ps1()
            for k in range(8):
                mm(PS[:, b, 0:wd], ONESB[:, :], BK[:, SQI[k], 0:wd], k == 0, k == 7,
                   ["ONESB", ("BK", SQI[k])], [("PS", b)])
            act(WK[:, 12, 0:wd], PS[:, b, 0:wd], AF.Ln, [("PS", b)], [("WK", 12)], scale=1.0 / D, bias=EPSB[:, 0:1])
            act(WK[:, 12, 0:wd], WK[:, 12, 0:wd], AF.Exp, [("WK", 12)], [("WK", 12)], scale=-0.5)
            for k in range(8):
                stt(HN[:, k, c0:c0 + wd], H[:, k, c0:c0 + wd], pp(l, off + k), WK[:, 12, 0:wd], ALU.mult, ALU.mult,
                    [("H", k), ("WK", 12), "PP"], [("HN", k, c0)])

        def norm_h(l, off, blocks):
            for blk in blocks:
                norm_p1(blk)
                norm_p2(l, off, blk)

        def lru_chunk(l, blk, c, st, sx, sg):
            (c0, wd, vf) = blk
            li = l * 4 + c
            W = lambda i: WK[:, st * 6 + i, 0:wd]
            wr = lambda i: ("WK", st * 6 + i)
            XSs = XS[:, st, :]
            xr = ("XS", st)
            xcb = BK[:, 8 + st, 0:wd]
            xbr = ("BK", 8 + st)
            hr = ("HIST", li)
            cr = ("CAR", li)
            bx, bg = yield from gps(2)
            for k in range(8):
                mm(PS[:, bx, 0:wd], RING[:, sx + c, k * 128:(k + 1) * 128], HN[:, k, c0:c0 + wd],
                   k == 0, k == 7, [("ring", sx + c), ("HN", k, c0)], [("PS", bx)])
            for k in range(8):
                mm(PS[:, bg, 0:wd], RING[:, sg + c, k * 128:(k + 1) * 128], HN[:, k, c0:c0 + wd],
                   k == 0, k == 7, [("ring", sg + c), ("HN", k, c0)], [("PS", bg)])
            yield
            act(XSs[:, 0:3], HIST[:, li, 0:3], AF.Copy, [hr], [xr])
            act(XSs[:, 3:3 + wd], PS[:, bx, 0:wd], AF.Copy, [("PS", bx)], [xr])
            ps_free(bx)
            act(HIST[:, li, 0:3], XSs[:, wd:wd + 3], AF.Copy, [xr], [hr])
            g0 = W(4)
            act(g0, PS[:, bg, 0:wd], AF.Square, [("PS", bg)], [wr(4)])
            yield
            xc = W(0)
            act(xc, XSs[:, 3:3 + wd], AF.Identity, [xr, "PP"], [wr(0)],
                scale=pp(l, CW + c * 4 + 3), bias=pp(l, CB + c))
            ts(g0, g0, 0.044715, 1.0, ALU.mult, ALU.add, [wr(4)], [wr(4)])
            yield
            for tap in (2, 1, 0):
                stt(xc, XSs[:, tap:tap + wd], pp(l, CW + c * 4 + tap), xc, ALU.mult, ALU.add,
                    [xr, wr(0), "PP"], [wr(0)])
                if tap == 2:
                    tt(g0, g0, PS[:, bg, 0:wd], ALU.mult, [wr(4), ("PS", bg)], [wr(4)])
                elif tap == 1:
                    act(g0, g0, AF.Tanh, [wr(4)], [wr(4)], scale=0.7978845608028654)
                else:
                    stt(g0, g0, 1.0, PS[:, bg, 0:wd], ALU.add, ALU.mult, [wr(4), ("PS", bg)], [wr(4)])
                    ps_free(bg)
                yield
            act(xcb, xc, AF.Copy, [wr(0)], [xbr])
            yield
            ba_, bb_ = yield from gps(2)
            gi = (c * 2) * 128
            mm(PS[:, ba_, 0:wd], GATES[:, gi:gi + 128], xcb, True, True, ["GATES", xbr], [("PS", ba_)])
            mm(PS[:, bb_, 0:wd], GATES[:, gi + 128:gi + 256], xcb, True, True, ["GATES", xbr], [("PS", bb_)])
            yield
            t1, t2, t3, t5 = W(1), W(2), W(3), W(5)
            act(t1, PS[:, ba_, 0:wd], AF.Tanh, [("PS", ba_), "DV"], [wr(1)], scale=0.5, bias=dv(l, HBA + c))
            act(t5, PS[:, bb_, 0:wd], AF.Tanh, [("PS", bb_), "DV"], [wr(5)], scale=0.5, bias=dv(l, HBX + c))
            ps_free(ba_, bb_)
            yield
            act(t2, t1, AF.Exp, [wr(1), "DV"], [wr(2)], scale=dv(l, N4 + c), bias=dv(l, N4 + c))
            act(t3, t1, AF.Tanh, [wr(1), "DV"], [wr(3)], scale=dv(l, P4 + c), bias=dv(l, P4 + c))
            stt(t5, t5, 1.0, xc, ALU.add, ALU.mult, [wr(5), wr(0)], [wr(5)])
            yield
            act(t1, t1, AF.Exp, [wr(1), "DV"], [wr(1)], scale=dv(l, N8 + c), bias=dv(l, N8 + c))
            yield
            stt(t1, t1, 1.0, t3, ALU.add, ALU.mult, [wr(1), wr(3)], [wr(1)])
            yield
            act(t1, t1, AF.Ln, [wr(1)], [wr(1)])
            act(t1, t1, AF.Exp, [wr(1)], [wr(1)], scale=0.5)
            yield
            stt(t5, t5, 0.5, t1, ALU.mult, ALU.mult, [wr(5), wr(1)], [wr(5)])
            yield
            if vf:
                P.op("dve", lambda e: e.memset(t1, 0.0), (), [wr(1)])
            P.op("dve", lambda e: e.tensor_tensor_scan(
                out=t1[:, vf:wd], data0=t2[:, vf:wd], data1=t5[:, vf:wd], initial=CAR[:, li:li + 1],
                op0=ALU.mult, op1=ALU.add), [wr(2), wr(5), cr], [wr(1)])
            yield
            act(CAR[:, li:li + 1], t1[:, wd - 1:wd], AF.Copy, [wr(1)], [cr])
            stt(BK[:, 24 + c, 0:wd], t1, 0.5, g0, ALU.mult, ALU.mult, [wr(1), wr(4)], [("BK", 24 + c)])
            yield
            if c == 3:
                for k in range(4):
                    act(BK[:, 14 + k, 0:wd], BK[:, 24 + k, 0:wd], AF.Square, [("BK", 24 + k)], [("BK", 14 + k)])
                yield
                b = yield from gps(1)
                for k in range(4):
                    mm(PS[:, b, 0:wd], ONESB[:, :], BK[:, 14 + k, 0:wd], k == 0, k == 3,
                       ["ONESB", ("BK", 14 + k)], [("PS", b)])
                yield
                act(WK[:, 12, 0:wd], PS[:, b, 0:wd], AF.Ln, [("PS", b)], [("WK", 12)], scale=1.0 / 512,
                    bias=EPSB[:, 0:1])
                ps_free(b)
                act(WK[:, 12, 0:wd], WK[:, 12, 0:wd], AF.Exp, [("WK", 12)], [("WK", 12)], scale=-0.5)
                yield
                for k in range(4):
                    stt(YM[:, k, c0:c0 + wd], BK[:, 24 + k, 0:wd], pp(l, OG + k), WK[:, 12, 0:wd], ALU.mult, ALU.mult,
                        [("BK", 24 + k), ("WK", 12), "PP"], [("YM", k)])
                    yield

        def ret_gate(l, blk, sgr):
            (c0, wd, vf) = blk
            yield "go"
            for hh in range(4):
                b = yield from gps(1)
                for k in range(8):
                    mm(PS[:, b, 0:wd], RING[:, sgr + hh, k * 128:(k + 1) * 128], HN[:, k, c0:c0 + wd],
                       k == 0, k == 7, [("ring", sgr + hh), ("HN", k, c0)], [("PS", b)])
                yield
                act(BK[:, 10 + hh, 0:wd], PS[:, b, 0:wd], AF.Tanh, [("PS", b)], [("BK", 10 + hh)], scale=0.5)
                yield
                stt(BK[:, 10 + hh, 0:wd], BK[:, 10 + hh, 0:wd], 1.0, PS[:, b, 0:wd], ALU.add, ALU.mult,
                    [("BK", 10 + hh), ("PS", b)], [("BK", 10 + hh)])
                ps_free(b)
                yield

        def ret_tile(l, lc, t0, gt, sq_, sk_, sv_, rs, tn, bc0):
            bi = lambda i: i if (rs == 0 or i < 2) else 16 + i
            B = lambda i: BK[:, bi(i), :]
            br = lambda i: ("BK", bi(i))
            SBr = SB2[:, (tn + 1) % 2, :]
            SBw = SB2[:, tn % 2, :]
            sbr_, sbw_ = ("SB2", (tn + 1) % 2), ("SB2", tn % 2)
            bqk = yield from gps(pair=True)
            for which, s0 in ((0, sq_), (1, sk_)):
                for k in range(8):
                    mm(PS[:, bqk + which, :], HN[:, k, lc:lc + 128], RING[:, s0:s0 + 4, k * 128:(k + 1) * 128],
                       k == 0, k == 7, [("HN", k, bc0)] + [("ring", s0 + j) for j in range(4)], [("PS", bqk + which)])
            bv = yield from gps(1)
            for k in range(8):
                mm(PS[:, bv, :], HN[:, k, lc:lc + 128], RING[:, sv_:sv_ + 4, k * 128:(k + 1) * 128],
                   k == 0, k == 7, [("HN", k, bc0)] + [("ring", sv_ + j) for j in range(4)], [("PS", bv)])
            yield
            act(B(2), PS[:, bv, :], AF.Copy, [("PS", bv)], [br(2)])
            ps_free(bv)
            qk = PS[:, bqk:bqk + 2, :].rearrange("p b (h t j) -> p (b h) t j", h=4, t=2, j=64)
            x1 = qk[:, :, 0, :]
            x2 = qk[:, :, 1, :]
            cs = COS(gt).unsqueeze(1).to_broadcast([128, 8, 64])
            sn = SIN(gt).unsqueeze(1).to_broadcast([128, 8, 64])
            ta = WK[:, 13, :].rearrange("p (h j) -> p h j", j=64)
            tb = WK[:, 14, :].rearrange("p (h j) -> p h j", j=64)
            qr = BK[:, 0:2, :].rearrange("p b (h t j) -> p (b h) t j", h=4, t=2, j=64)
            rqk = [("PS", bqk), ("PS", bqk + 1), "CS"]
            tt(ta, x1, cs, ALU.mult, rqk, [("WK", 13)])
            yield
            tt(tb, x2, sn, ALU.mult, rqk, [("WK", 14)])
            yield
            tt(qr[:, :, 0, :], ta, tb, ALU.subtract, [("WK", 13), ("WK", 14)], [("BK", 0), ("BK", 1)])
            yield
            tt(ta, x1, sn, ALU.mult, rqk, [("WK", 13)])
            yield
            tt(tb, x2, cs, ALU.mult, rqk, [("WK", 14)])
            ps_free(bqk, bqk + 1)
            yield
            tt(qr[:, :, 1, :], ta, tb, ALU.add, [("WK", 13), ("WK", 14)], [("BK", 0), ("BK", 1)])
            yield
            tt(B(3), BK[:, 1, :], ZETA, ALU.mult, [("BK", 1), "CST"], [br(3)])
            pqk = yield from gpsb()
            for hh in range(4):
                tr(PSB[:, pqk, hh * 128:(hh + 1) * 128], BK[:, 0, hh * 128:(hh + 1) * 128], IDB[:, :],
                   [("BK", 0), "IDB"], [("PSB", pqk)])
            for hh in range(4):
                tr(PSB[:, pqk, 512 + hh * 128:512 + (hh + 1) * 128], BK[:, 1, hh * 128:(hh + 1) * 128], IDB[:, :],
                   [("BK", 1), "IDB"], [("PSB", pqk)])
            yield
            bS = yield from gps(1)
            for hh in range(4):
                sl = slice(hh * 128, (hh + 1) * 128)
                mm(PS[:, bS, sl], B(3)[:, sl], B(2)[:, sl], True, True, [br(3), br(2)], [("PS", bS)])
            tt(B(4), PSB[:, pqk, 0:512], XI, ALU.mult, [("PSB", pqk), "CST"], [br(4)])
            act(B(5), PSB[:, pqk, 512:1024], AF.Copy, [("PSB", pqk), br(4)], [br(5)])
            psb_free(pqk)
            yield
            for hh in range(4):
                sl = slice(hh * 128, (hh + 1) * 128)
                stt(SF[:, l, sl], SF[:, l, sl], GAM[hh] ** 128, PS[:, bS, sl], ALU.mult, ALU.add,
                    ["SF", ("PS", bS)], ["SF"])
            ps_free(bS)
            yield
            act(SBw, SF[:, l, :], AF.Copy, ["SF"], [sbw_])
            yield "go"
            bs = yield from gps(1)
            for hh in range(4):
                sl = slice(hh * 128, (hh + 1) * 128)
                mm(PS[:, bs, sl], B(5)[:, sl], B(4)[:, sl], True, True, [br(5), br(4)], [("PS", bs)])
            yield
            tt(B(6), PS[:, bs, :], MASK, ALU.mult, [("PS", bs), "CST"], [br(6)])
            ps_free(bs)
            yield
            by = yield from gps(1)
            for hh in range(4):
                sl = slice(hh * 128, (hh + 1) * 128)
                mm(PS[:, by, sl], B(6)[:, sl], B(2)[:, sl], True, False, [br(6), br(2)], [("PS", by)])
                mm(PS[:, by, sl], B(4)[:, sl], SBr[:, sl], False, True, [br(4), sbr_], [("PS", by)])
            yield
            for hh in range(4):
                sl = slice(hh * 128, (hh + 1) * 128)
                P.op("dve", lambda e, hh=hh, sl=sl: e.bn_stats(out=STAT[:, rs, hh, 0:6], in_=PS[:, by, sl]),
                     [("PS", by)], [("STAT", rs, hh)])
                yield
            for hh in range(4):
                P.op("dve", lambda e, hh=hh: e.bn_aggr(out=MV[:, rs, hh, :], in_=STAT[:, rs, hh, 0:6]),
                     [("STAT", rs, hh)], [("MV", rs)])
            yield
            act(RSTD[:, rs * 4:rs * 4 + 4], MV[:, rs, :, 1], AF.Ln, [("MV", rs)], [("RSTD", rs)], bias=EPSB[:, 0:1])
            act(RSTD[:, rs * 4:rs * 4 + 4], RSTD[:, rs * 4:rs * 4 + 4], AF.Exp, [("RSTD", rs)], [("RSTD", rs)],
                scale=-0.5)
            yield
            for hh in range(4):
                sl = slice(hh * 128, (hh + 1) * 128)
                ts(B(7)[:, sl], PS[:, by, sl], MV[:, rs, hh, 0:1], RSTD[:, rs * 4 + hh:rs * 4 + hh + 1],
                   ALU.subtract, ALU.mult, [("PS", by), ("MV", rs), ("RSTD", rs)], [br(7)])
            ps_free(by)
            yield
            py = yield from gpsb()
            for hh in range(4):
                tr(PSB[:, py, hh * 128:(hh + 1) * 128], B(7)[:, hh * 128:(hh + 1) * 128], IDB[:, :],
                   [br(7), "IDB"], [("PSB", py)])
            yield
            for hh in range(4):
                stt(YM[:, 4 + hh, lc:lc + 128], PSB[:, py, hh * 128:(hh + 1) * 128], dv(l, HRG + hh),
                    BK[:, 10 + hh, t0:t0 + 128], ALU.mult, ALU.mult,
                    [("PSB", py), "DV", ("BK", 10 + hh)], [("YM", 4 + hh)])
                if hh == 3:
                    psb_free(py)
                yield

        class Stream:
            def __init__(self, groups, width, gated=False):
                self.groups = groups
                self.gi = 0
                self.cur = []
                self.active = []
                self.width = width
                self.gated = gated
                self.open = True

            def done_groups(self):
                return self.gi - 1 if (self.active or self.cur) else self.gi

            def step(self):
                if not self.active and not self.cur:
                    if self.gi >= len(self.groups):
                        return False
                    self.cur = list(self.groups[self.gi])
                    self.gi += 1
                    self.open = True
                while len(self.active) < self.width and self.cur and (self.open or not self.gated):
                    self.active.append(self.cur.pop(0))
                    self.open = False
                for g in list(self.active):
                    try:
                        r = next(g)
                        if r == "go":
                            self.open = True
                    except StopIteration:
                        self.active.remove(g)
                        if g is getattr(self, "last", None):
                            pass
                return True

        def mixer_phase(l, blocks, hc0):
            rp[0] = 0
            sx = ring_alloc(4)
            for c in range(4):
                load_cols(win_d, l, c * 128, sx + c)
            sg = ring_alloc(4)
            for c in range(4):
                load_cols(win_d, l, 512 + c * 128, sg + c)
            P.dma("pool", lambda e: e.dma_start(out=GATES[:, :], in_=gates_d[:, l * 1024:(l + 1) * 1024]),
                  SEM_SETUP + 3, (), ["GATES"])
            sq_ = ring_alloc(4, 4)
            sk_ = ring_alloc(4, 4)
            sv_ = ring_alloc(4, 4)
            sgr = ring_alloc(4, 4)
            for base, s0 in ((1024, sq_), (1536, sk_), (2048, sv_), (2560, sgr)):
                for c in range(4):
                    load_cols(win_d, l, base + c * 128, s0 + c)
            act(SB2[:, 1, :], SF[:, l, :], AF.Copy, ["SF"], [("SB2", 1)])
            lru_groups = []
            ui = 0
            for blk in blocks:
                g = []
                for c in range(4):
                    g.append(lru_chunk(l, blk, c, ui % 2, sx, sg))
                    ui += 1
                lru_groups.append(g)
            ret_groups = []
            tn = 0
            for blk in blocks:
                (c0, wd, vf) = blk
                g = [ret_gate(l, blk, sgr)]
                for t0 in range(0, wd, 128):
                    lc = c0 + t0
                    g.append(ret_tile(l, lc, t0, (hc0 + lc) // 128, sq_, sk_, sv_, tn % 2, tn, c0))
                    tn += 1
                ret_groups.append(g)
            s_ret = Stream(ret_groups, 2, gated=True)
            s_lru = Stream(lru_groups, 2)

            def out_stream():
                while s_lru.done_groups() < len(blocks):
                    yield
                so = ring_alloc(8)
                for dc in range(8):
                    load_cols(wout_d, l, dc * 128, so + dc)
                for i, blk in enumerate(blocks):
                    (c0, wd, vf) = blk
                    while s_ret.done_groups() <= i:
                        yield
                    for dc in range(8):
                        b = yield from gps(1)
                        for k in range(8):
                            mm(PS[:, b, 0:wd], RING[:, so + dc, k * 128:(k + 1) * 128], YM[:, k, c0:c0 + wd],
                               k == 0, k == 7, [("ring", so + dc), ("YM", k)], [("PS", b)])
                        yield
                        resid_add(b, dc, c0, wd, vf)
                        ps_free(b)
                        yield
                    norm_p1(blk)
                    yield
                    b = yield from gps(1)
                    norm_p2(l, NF, blk, b)
                    ps_free(b)
                    yield

            streams = [s_ret, s_lru, Stream([[out_stream()]], 1)]
            alive = True
            while alive:
                alive = False
                for st in streams:
                    if st.step():
                        alive = True

        def resid_add(b, dc, c0, wd, vf):
            tt(H[:, dc, c0 + vf:c0 + wd], PS[:, b, vf:wd], H[:, dc, c0 + vf:c0 + wd], ALU.add,
               [("PS", b), ("H", dc)], [("H", dc)])

        def out_phase(l, blocks):
            so = ring_alloc(8)
            for dc in range(8):
                load_cols(wout_d, l, dc * 128, so + dc)
            for i, (c0, wd, vf) in enumerate(blocks):
                for dc in range(8):
                    b = ps1()
                    for k in range(8):
                        mm(PS[:, b, 0:wd], RING[:, so + dc, k * 128:(k + 1) * 128], YM[:, k, c0:c0 + wd],
                           k == 0, k == 7, [("ring", so + dc), ("YM", k)], [("PS", b)])
                    resid_add(b, dc, c0, wd, vf)
                if i > 0:
                    norm_p2(l, NF, blocks[i - 1])
                norm_p1(blocks[i])
            norm_p2(l, NF, blocks[-1])

        def ffn_phase(l, blocks, next_norm):
            j0 = 0
            for gidx, ng in enumerate(FFG):
                sgs, sus, sds = [], [], []
                for j in range(j0, j0 + ng):
                    s = ring_alloc(3)
                    load_cols(wg_d, l, j * 128, s)
                    load_cols(wu_d, l, j * 128, s + 1)
                    load_rows(wd_d, l, j * 128, s + 2)
                    sgs.append(s)
                    sus.append(s + 1)
                    sds.append(s + 2)
                for bi, (c0, wd, vf) in enumerate(blocks):
                    for jj in range(ng):
                        bg = ps1()
                        bu = ps1()
                        for k in range(8):
                            mm(PS[:, bg, 0:wd], RING[:, sgs[jj], k * 128:(k + 1) * 128], HN[:, k, c0:c0 + wd],
                               k == 0, k == 7, [("ring", sgs[jj]), ("HN", k, c0)], [("PS", bg)])
                        for k in range(8):
                            mm(PS[:, bu, 0:wd], RING[:, sus[jj], k * 128:(k + 1) * 128], HN[:, k, c0:c0 + wd],
                               k == 0, k == 7, [("ring", sus[jj]), ("HN", k, c0)], [("PS", bu)])
                        wi = jj % 2
                        t = WK[:, wi, 0:wd]
                        act(t, PS[:, bg, 0:wd], AF.Silu, [("PS", bg)], [("WK", wi)])
                        ai = (bi % 2) * 4 + jj
                        tt(BK[:, ai, 0:wd], t, PS[:, bu, 0:wd], ALU.mult, [("WK", wi), ("PS", bu)], [("BK", ai)])
                    for dc in range(8):
                        b = ps1()
                        for jj in range(ng):
                            ai = (bi % 2) * 4 + jj
                            mm(PS[:, b, 0:wd], RING[:, sds[jj], dc * 128:(dc + 1) * 128], BK[:, ai, 0:wd],
                               jj == 0, jj == ng - 1, [("ring", sds[jj]), ("BK", ai)], [("PS", b)])
                        resid_add(b, dc, c0, wd, vf)
                    if next_norm and gidx == len(FFG) - 1:
                        if bi > 0:
                            norm_p2(l + 1, NM, blocks[bi - 1])
                        norm_p1(blocks[bi])
                if next_norm and gidx == len(FFG) - 1:
                    norm_p2(l + 1, NM, blocks[-1])
                j0 += ng

        for hi, (t_lo, t_hi) in enumerate(HALVES):
            hc0 = t_lo * 128
            ntile = t_hi - t_lo
            if hi == 0:
                blocks = [(0, 384, PADF), (384, 384, 0), (768, 384, 0)]
            else:
                blocks = [(0, 512, 0), (512, 512, 0)]
            tlo[0] = t_lo
            P.dma("sp", lambda e, hi=hi: e.dma_start(out=CS[:, :, :].rearrange("p a f -> p (a f)"),
                                                      in_=cs_d[:, hi * 1152:(hi + 1) * 1152]),
                  SEM_SETUP + 2, (), ["CS"])
            for ti in range(ntile):
                gt = t_lo + ti
                wb = (ti % 2) * 2
                xst = WK[:, wb:wb + 2, :].rearrange("p a f -> p (a f)")
                regs = [("WK", wb), ("WK", wb + 1)]
                if gt == 0:
                    P.op("dve", lambda e, xst=xst: e.memset(xst, 0.0), (), regs)
                    P.dma("sp", lambda e, xst=xst: e.dma_start(out=xst[PADF:128, :], in_=meta_d), SEM_XST[ti % 2],
                          (), regs)
                else:
                    r0 = (gt - 1) * 128
                    P.dma("sp", lambda e, xst=xst, r0=r0: e.dma_start(out=xst, in_=x_d[r0:r0 + 128, :]),
                          SEM_XST[ti % 2], (), regs)
                b = ps2()
                for k in range(8):
                    bb = b + k // 4
                    tr(PS[:, bb, (k % 4) * 128:(k % 4 + 1) * 128], xst[:, k * 128:(k + 1) * 128], IDENT,
                       regs + ["CST"], [("PS", bb)])
                for k in range(8):
                    bb = b + k // 4
                    act(H[:, k, ti * 128:(ti + 1) * 128], PS[:, bb, (k % 4) * 128:(k % 4 + 1) * 128], AF.Copy,
                        [("PS", bb)], [("H", k)])
            for l in range(DEPTH):
                if l == 0:
                    norm_h(l, NM, blocks)
                mixer_phase(l, blocks, hc0)
                if DBG is not None and DBG == (l, hi):
                    P.dma("sp", lambda e: e.dma_start(out=dbg_d, in_=YM[:, :, :].rearrange("p a f -> p (a f)")),
                          SEM_SETUP + 1, [("YM", k) for k in range(8)], ())
                ffn_phase(l, blocks, l < DEPTH - 1)
            P.dma("sp", lambda e: e.dma_start(out=GFIN, in_=gfin_d), SEM_SETUP + 3 + 1, (), [("WK", 8), ("WK", 9)])
            for ti in range(ntile):
                gt = t_lo + ti
                if gt == 0:
                    continue
                b = ps2()
                for k in range(8):
                    bb = b + k // 4
                    tr(PS[:, bb, (k % 4) * 128:(k % 4 + 1) * 128], H[:, k, ti * 128:(ti + 1) * 128], IDENT,
                       [("H", k), "CST"], [("PS", bb)])
                wb = (ti % 2) * 2
                ho = WK[:, wb:wb + 2, :].rearrange("p a f -> p (a f)")
                regs = [("WK", wb), ("WK", wb + 1)]
                for a in range(2):
                    act(ho[:, a * 512:(a + 1) * 512], PS[:, b + a, :], AF.Copy, [("PS", b + a)], regs)
                ob = 4 + (ti % 2) * 2
                ot = WK[:, ob:ob + 2, :].rearrange("p a f -> p (a f)")
                oregs = [("WK", ob), ("WK", ob + 1)]
                act(ot, ho, AF.Square, regs, oregs + ["RSTD"], accum=RSTD[:, 8:9])
                act(RSTD[:, 8:9], RSTD[:, 8:9], AF.Ln, ["RSTD"], ["RSTD"], scale=1.0 / D, bias=EPSB[:, 0:1])
                act(RSTD[:, 8:9], RSTD[:, 8:9], AF.Exp, ["RSTD"], ["RSTD"], scale=-0.5)
                stt(ot, ho, RSTD[:, 8:9], GFIN, ALU.mult, ALU.mult, regs + ["RSTD", ("WK", 8), ("WK", 9)], oregs)
                r0 = (gt - 1) * 128
                P.dma("sp", lambda e, ot=ot, r0=r0: e.dma_start(out=out_d[r0:r0 + 128, :], in_=ot),
                      SEM_OUT[ti % 2], oregs, ())
        P.op("sp", None, (), [("WK", i) for i in range(4, 8)])

        P.finalize()
        with nc.Block() as block:
            @block.tensor
            def _(e):
                P.emit("pe", e, esems, dsems)

            @block.scalar
            def _(e):
                P.emit("act", e, esems, dsems)

            @block.vector
            def _(e):
                P.emit("dve", e, esems, dsems)

            @block.gpsimd
            def _(e):
                P.emit("pool", e, esems, dsems)

            @block.sync
            def _(e):
                P.emit("sp", e, esems, dsems)
    return nc


def _host_tables():
    pos = np.maximum(np.arange(TP, dtype=np.float32) - PADF, 0.0).astype(np.float32)
    half = 64
    inv = (np.float32(10000.0) ** (-np.arange(half, dtype=np.float32) / np.float32(half))).astype(np.float32)
    ang = (pos[:, None] * inv[None, :]).astype(np.float32)
    cos = np.cos(ang).astype(np.float32).reshape(NT, 128, 64).transpose(1, 0, 2)
    sin = np.sin(ang).astype(np.float32).reshape(NT, 128, 64).transpose(1, 0, 2)
    cs = np.zeros((128, 2, 2, 9, 64), np.float32)
    for hi, (a, b) in enumerate(HALVES):
        cs[:, hi, 0, 0:b - a] = cos[:, a:b]
        cs[:, hi, 1, 0:b - a] = sin[:, a:b]
    idx = np.arange(128, dtype=np.float64)
    mask = np.zeros((128, 4, 128), np.float32)
    xi = np.zeros((128, 4, 128), np.float32)
    zeta = np.zeros((128, 4, 128), np.float32)
    for h in range(4):
        g = GAM[h]
        m = np.where(idx[None, :] >= idx[:, None], (g ** (-(idx + 1.0)))[:, None], 0.0)
        mask[:, h, :] = m
        xi[:, h, :] = ((g ** (idx + 1.0)) * (128.0 ** -0.5))[None, :]
        zeta[:, h, :] = (g ** (127.0 - idx))[:, None]
    ident = np.eye(128, dtype=np.float32)
    cst = np.concatenate([ident, mask.reshape(128, 512), xi.reshape(128, 512), zeta.reshape(128, 512)], axis=1)
    return np.ascontiguousarray(cst.astype(np.float32)), np.ascontiguousarray(cs.reshape(128, -1))


_NC_CACHE = {}


def kernel(x, meta_tokens, norm_mix, w_in, conv_w, conv_b, gate_a_w, gate_a_b, gate_x_w, gate_x_b,
           lru_lambda, lru_out_norm, ret_out_norm, w_out, norm_ffn, w_gate, w_up, w_down, norm_final):
    f = lambda a: np.ascontiguousarray(np.asarray(a, dtype=np.float32))
    x = f(x)
    pp = np.zeros((128, DEPTH * PL), np.float32)
    gates = np.zeros((128, DEPTH * 4 * 2 * 128), np.float32)
    gaw, gxw = f(gate_a_w), f(gate_x_w)
    for l in range(DEPTH):
        o = l * PL
        pp[:, o + NM:o + NM + 8] = f(norm_mix)[l].reshape(8, 128).T
        pp[:, o + NF:o + NF + 8] = f(norm_ffn)[l].reshape(8, 128).T
        cw = f(conv_w)[l]
        for c in range(4):
            pp[:, o + CW + c * 4:o + CW + c * 4 + 4] = cw[:, c * 128:(c + 1) * 128].T
        pp[:, o + CB:o + CB + 4] = f(conv_b)[l].reshape(4, 128).T
        pp[:, o + BA:o + BA + 4] = f(gate_a_b)[l].reshape(4, 128).T
        pp[:, o + BX:o + BX + 4] = f(gate_x_b)[l].reshape(4, 128).T
        pp[:, o + LAM:o + LAM + 4] = f(lru_lambda)[l].reshape(4, 128).T
        pp[:, o + OG:o + OG + 4] = f(lru_out_norm)[l].reshape(4, 128).T
        pp[:, o + RG:o + RG + 4] = f(ret_out_norm)[l].reshape(4, 128).T
        for c in range(4):
            for wi, gw in enumerate((gaw, gxw)):
                base = ((l * 4 + c) * 2 + wi) * 128
                gates[0:64, base:base + 64] = gw[l, 2 * c]
                gates[64:128, base + 64:base + 128] = gw[l, 2 * c + 1]
    cst, cs = _host_tables()
    gfin = np.ascontiguousarray(np.broadcast_to(f(norm_final)[None, :], (128, D)))
    if "nc" not in _NC_CACHE:
        _NC_CACHE["nc"] = build_nc()
    nc = _NC_CACHE["nc"]
    shared = {"meta": f(meta_tokens), "w_in": f(w_in), "w_out": f(w_out), "w_gate": f(w_gate), "w_up": f(w_up),
              "w_down": f(w_down), "gates": gates, "pp": pp, "cst": cst, "cs": cs, "gfin": gfin}
    in_maps = [dict(shared, x=x[b]) for b in range(8)]
    res = run_bass_kernel_spmd(nc, in_maps, core_ids=list(range(8)))
    if DBG is not None:
        DBG_OUT["dbg"] = np.asarray(res.results[0]["dbg"])
    return np.stack([np.asarray(r["out"], dtype=np.float32) for r in res.results], axis=0)
```

```python
import contextlib
import numpy as np
import concourse.bass as bass
import concourse.mybir as mybir
from concourse.bass_utils import run_bass_kernel_spmd

F32 = mybir.dt.float32
BF16 = mybir.dt.bfloat16
AF = mybir.ActivationFunctionType
ALU = mybir.AluOpType

D = 1024
SEQ = 2048
NMETA = 16
DEPTH = 4
DFF = 2816
NFF = DFF // 128
PADF = 112
TP = 2176
NT = TP // 128
HALVES = [(0, 9), (9, 17)]
HW = 9 * 128
RS = 24
EPS = 1e-6
SELF_SYNC = True
DBG = None
DBG_OUT = {}

NM, NF, CW, CB, BA, BX, LAM, OG, RG, PL = 0, 8, 16, 32, 36, 40, 44, 48, 52, 56
HBA, HBX, P4, N4, N8, HRG, DL = 0, 4, 8, 12, 16, 20, 24
GAM = [1.0 - 2.0 ** (-5.0 - h) for h in range(4)]
FFG = [4, 4, 4, 4, 3, 3]


class Prog:
    def __init__(self):
        self.ops = {k: [] for k in ("pe", "act", "dve", "pool", "sp")}
        self.last_w = {}
        self.readers = {}
        self.dma_cnt = {}
        self.touch = {}
        self.clock = 0

    def _deps(self, reads, writes, tok):
        deps = []
        self.clock += 1
        for r in list(reads) + list(writes):
            self.touch[r] = self.clock
        for r in reads:
            t = self.last_w.get(r)
            if t is not None:
                deps.append(t)
        for w in writes:
            t = self.last_w.get(w)
            if t is not None:
                deps.append(t)
            deps.extend(self.readers.get(w, ()))
        for r in reads:
            self.readers.setdefault(r, []).append(tok)
        for w in writes:
            self.last_w[w] = tok
            self.readers[w] = []
        return deps

    def op(self, eng, fn, reads=(), writes=()):
        lst = self.ops[eng]
        tok = ("E", eng, len(lst))
        deps = self._deps(reads, writes, tok)
        lst.append({"fn": fn, "deps": deps, "tok": tok, "inc": False})

    def dma(self, q, fn, sem, reads=(), writes=()):
        self.dma_cnt[sem] = self.dma_cnt.get(sem, 0) + 16
        tok = ("D", sem, self.dma_cnt[sem])
        deps = self._deps(reads, writes, tok)
        self.ops[q].append({"fn": fn, "deps": deps, "tok": tok, "inc": False, "dsem": sem})

    def finalize(self):
        for eng, lst in self.ops.items():
            for o in lst:
                nd = []
                for d in set(o["deps"]):
                    if d[0] == "E":
                        if d[1] == eng and (eng == "pe" or not SELF_SYNC):
                            continue
                        if d == o["tok"]:
                            continue
                        self.ops[d[1]][d[2]]["inc"] = True
                    nd.append(d)
                o["deps"] = nd
        self.cnt = {}
        for eng, lst in self.ops.items():
            c = 0
            arr = []
            for o in lst:
                if o["inc"]:
                    c += 1
                arr.append(c)
            self.cnt[eng] = arr

    def emit(self, eng, e, esems, dsems):
        known = {}
        for o in self.ops[eng]:
            need = {}
            for d in o["deps"]:
                if d[0] == "E":
                    key = ("E", d[1])
                    val = self.cnt[d[1]][d[2]]
                else:
                    key = ("D", d[1])
                    val = d[2]
                if val > known.get(key, 0):
                    need[key] = max(need.get(key, 0), val)
            for key, val in need.items():
                sem = esems[key[1]] if key[0] == "E" else dsems[key[1]]
                e.wait_ge(sem, val)
                known[key] = val
            if o["fn"] is None:
                continue
            ins = o["fn"](e)
            if "dsem" in o:
                ins.then_inc(dsems[o["dsem"]], 16)
            elif o["inc"]:
                ins.then_inc(esems[eng], 1)


def build_nc():
    nc = bass.Bass("TRN2", target_bir_lowering=False)
    dt = lambda n, s: nc.dram_tensor(n, s, F32, kind="ExternalInput").ap()
    x_d = dt("x", [SEQ, D])
    meta_d = dt("meta", [NMETA, D])
    win_d = dt("w_in", [DEPTH, D, 3072])
    wout_d = dt("w_out", [DEPTH, D, D])
    wg_d = dt("w_gate", [DEPTH, D, DFF])
    wu_d = dt("w_up", [DEPTH, D, DFF])
    wd_d = dt("w_down", [DEPTH, DFF, D])
    gates_d = dt("gates", [128, DEPTH * 4 * 2 * 128])
    pp_d = dt("pp", [128, DEPTH * PL])
    cst_d = dt("cst", [128, 128 + 3 * 512])
    cs_d = dt("cs", [128, 2 * 2 * 576])
    gfin_d = dt("gfin", [128, D])
    out_d = nc.dram_tensor("out", [SEQ, D], F32, kind="ExternalOutput").ap()
    if DBG is not None:
        dbg_d = nc.dram_tensor("dbg", [128, 8 * HW], BF16, kind="ExternalOutput").ap()

    P = Prog()
    es = contextlib.ExitStack()
    with es:
        sb = lambda n, s, d=F32: es.enter_context(nc.sbuf_tensor(n, s, d))
        H = sb("H", [128, 8, HW])
        HN = sb("HN", [128, 8, HW], BF16)
        YM = sb("YM", [128, 8, HW], BF16)
        RING = sb("RING", [128, RS, 1024], BF16)
        CST = sb("CST", [128, 128 + 3 * 512])
        CS = sb("CS", [128, 2, 576])
        GATES = sb("GATES", [128, 1024], BF16)
        PP = sb("PP", [128, DEPTH * PL])
        DV = sb("DV", [128, DEPTH * DL])
        TMPS = sb("TMPS", [128, 3, 16])
        IDB = sb("IDB", [128, 128], BF16)
        ONESB = sb("ONESB", [128, 128], BF16)
        SF = sb("SF", [128, DEPTH, 512])
        SB2 = sb("SB2", [128, 2, 512], BF16)
        CAR = sb("CAR", [128, DEPTH * 4])
        HIST = sb("HIST", [128, DEPTH * 4, 4])
        XS = sb("XS", [128, 2, 520])
        WK = sb("WK", [128, 15, 512])
        BK = sb("BK", [128, 28, 512], BF16)
        STAT = sb("STAT", [128, 2, 4, 8])
        MV = sb("MV", [128, 2, 4, 2])
        RSTD = sb("RSTD", [128, 12])
        PS = es.enter_context(nc.psum_tensor("PS", [128, 6, 512], F32))
        PSB = es.enter_context(nc.psum_tensor("PSB", [128, 2, 1024], BF16))
        esems = {k: es.enter_context(nc.semaphore("s_" + k)) for k in ("pe", "act", "dve")}
        ND = RS + 9
        dsems = [es.enter_context(nc.semaphore("d%d" % i)) for i in range(ND)]
        SEM_XST = [RS, RS + 1]
        SEM_OUT = [RS + 2, RS + 3]
        SEM_SETUP = RS + 4

        IDENT = CST[:, 0:128]
        MASK = CST[:, 128:640]
        XI = CST[:, 640:1152]
        ZETA = CST[:, 1152:1664]
        GFIN = WK[:, 8:10, :].rearrange("p a f -> p (a f)")
        tlo = [0]

        def COS(gt):
            return CS[:, 0, (gt - tlo[0]) * 64:(gt - tlo[0] + 1) * 64]

        def SIN(gt):
            return CS[:, 1, (gt - tlo[0]) * 64:(gt - tlo[0] + 1) * 64]

        def act(out, in_, func, r, w, scale=None, bias=None, accum=None):
            kw = {}
            if scale is not None:
                kw["scale"] = scale
            if bias is not None:
                kw["bias"] = bias
            if accum is not None:
                kw["accum_out"] = accum
            P.op("act", lambda e: e.activation(out=out, in_=in_, func=func, **kw), r, w)

        def tt(out, in0, in1, op, r, w, eng="dve"):
            P.op(eng, lambda e: e.tensor_tensor(out=out, in0=in0, in1=in1, op=op), r, w)

        def stt(out, in0, scalar, in1, op0, op1, r, w):
            P.op("dve", lambda e: e.scalar_tensor_tensor(out=out, in0=in0, scalar=scalar, in1=in1,
                                                         op0=op0, op1=op1), r, w)

        def ts(out, in0, s1, s2, op0, op1, r, w):
            if op1 is None:
                P.op("dve", lambda e: e.tensor_scalar(out=out, in0=in0, scalar1=s1, scalar2=None, op0=op0), r, w)
            else:
                P.op("dve", lambda e: e.tensor_scalar(out=out, in0=in0, scalar1=s1, scalar2=s2,
                                                      op0=op0, op1=op1), r, w)

        def mm(out, lhsT, rhs, start, stop, r, w):
            P.op("pe", lambda e: e.matmul(out, lhsT, rhs, start=start, stop=stop), r, w)

        def tr(out, in_, ident, r, w):
            P.op("pe", lambda e: e.transpose(out, in_, ident), r, w)

        ps_busy = [False] * 6

        def ps_try(n=1, pair=False):
            tch = lambda i: P.touch.get(("PS", i), -1)
            if pair:
                c = [i for i in (0, 2, 4) if not ps_busy[i] and not ps_busy[i + 1]]
                if not c:
                    return None
                b = min(c, key=lambda i: max(tch(i), tch(i + 1)))
                ps_busy[b] = ps_busy[b + 1] = True
                P.clock += 1
                P.touch[("PS", b)] = P.touch[("PS", b + 1)] = P.clock
                return b
            free = sorted([i for i in range(6) if not ps_busy[i]], key=tch)
            if len(free) < n:
                return None
            for b in free[:n]:
                ps_busy[b] = True
                P.clock += 1
                P.touch[("PS", b)] = P.clock
            return free[0] if n == 1 else free[:n]

        def ps_free(*bs):
            for b in bs:
                ps_busy[b] = False

        def gps(n=1, pair=False):
            while True:
                r = ps_try(n, pair)
                if r is not None:
                    return r
                yield

        def ps1():
            b = ps_try(1)
            assert b is not None
            ps_busy[b] = False
            return b

        def ps2():
            b = ps_try(pair=True)
            assert b is not None
            ps_busy[b] = ps_busy[b + 1] = False
            return b

        psb_busy = [False, False]

        def gpsb():
            while True:
                for i in (0, 1):
                    if not psb_busy[i]:
                        psb_busy[i] = True
                        return i
                yield

        def psb_free(i):
            psb_busy[i] = False

        psbi = [0]

        def psb1():
            b = psbi[0] % 2
            psbi[0] += 1
            return b

        fifo = list(range(RS))

        def take(n):
            assert len(fifo) >= n
            out = fifo[:n]
            del fifo[:n]
            return out

        def take_run4():
            for s0 in list(fifo):
                if s0 + 3 < RS and all((s0 + j) in fifo for j in range(4)):
                    for j in range(4):
                        fifo.remove(s0 + j)
                    return s0
            raise AssertionError("no contiguous run")

        def release(slots):
            fifo.extend(slots)

        def load_cols(w3, l, c0, s):
            src = w3[l].rearrange("(kc p) f -> p kc f", p=128)[:, :, c0:c0 + 128]
            dst = RING[:, s, :].rearrange("p (kc f) -> p kc f", kc=8)
            P.dma("pool", lambda e: e.dma_start(out=dst, in_=src), s, (), [("ring", s)])

        def load_rows(w3, l, r0, s):
            src = w3[l][r0:r0 + 128, :]
            dst = RING[:, s, :]
            P.dma("pool", lambda e: e.dma_start(out=dst, in_=src), s, (), [("ring", s)])

        P.dma("sp", lambda e: e.dma_start(out=CST[:, :], in_=cst_d), SEM_SETUP, (), ["CST"])
        P.dma("sp", lambda e: e.dma_start(out=PP[:, :], in_=pp_d), SEM_SETUP + 1, (), ["PP"])
        act(IDB[:, :], IDENT, AF.Copy, ["CST"], ["IDB"])
        P.op("dve", lambda e: e.memset(ONESB[:, :], 1.0), (), ["ONESB"])
        P.op("dve", lambda e: e.memset(SF[:, :, :], 0.0), (), ["SF"])
        P.op("dve", lambda e: e.memset(CAR[:, :], 0.0), (), [("CAR", i) for i in range(16)])
        P.op("dve", lambda e: e.memset(HIST[:, :, :], 0.0), (), [("HIST", i) for i in range(16)])
        P.op("dve", lambda e: e.memset(WK[:, :, :], 0.0), (), [("WK", i) for i in range(15)])
        P.op("dve", lambda e: e.memset(XS[:, :, :], 0.0), (), [("XS", 0), ("XS", 1)])
        PPv = PP[:, :].rearrange("p (l c) -> p l c", c=PL)
        DVv = DV[:, :].rearrange("p (l c) -> p l c", c=DL)
        T0 = TMPS[:, 0, :].rearrange("p (l c) -> p l c", c=4)
        T1 = TMPS[:, 1, :].rearrange("p (l c) -> p l c", c=4)
        act(DVv[:, :, HBA:HBA + 4], PPv[:, :, BA:BA + 4], AF.Copy, ["PP"], ["DV"], scale=0.5)
        act(DVv[:, :, HBX:HBX + 4], PPv[:, :, BX:BX + 4], AF.Copy, ["PP"], ["DV"], scale=0.5)
        act(DVv[:, :, HRG:HRG + 4], PPv[:, :, RG:RG + 4], AF.Copy, ["PP"], ["DV"], scale=0.5)
        act(T0, PPv[:, :, LAM:LAM + 4], AF.Exp, ["PP"], ["T0"], scale=-1.0)
        nterm = 8
        cf = [0.0] + [((-1.0) ** (k + 1)) / k for k in range(1, nterm + 1)]
        ts(T1, T0, cf[nterm], cf[nterm - 1], ALU.mult, ALU.add, ["T0"], ["T1"])
        for k in range(nterm - 2, 0, -1):
            tt(T1, T1, T0, ALU.mult, ["T1", "T0"], ["T1"])
            ts(T1, T1, cf[k], None, ALU.add, None, ["T1"], ["T1"])
        tt(T1, T1, T0, ALU.mult, ["T1", "T0"], ["T1"])
        ts(DVv[:, :, P4:P4 + 4], T1, 4.0, None, ALU.mult, None, ["T1"], ["DV"])
        ts(DVv[:, :, N4:N4 + 4], T1, -4.0, None, ALU.mult, None, ["T1"], ["DV"])
        ts(DVv[:, :, N8:N8 + 4], T1, -8.0, None, ALU.mult, None, ["T1"], ["DV"])

        def pp(l, off):
            return PP[:, l * PL + off: l * PL + off + 1]

        def dv(l, off):
            return DV[:, l * DL + off: l * DL + off + 1]

        def rms_feature(src_fn, nk, gain_fn, out_fn, blocks, srcreg, outreg, inv_n):
            for (c0, wd, vf) in blocks:
                sq = BK[:, 0:nk, 0:wd]
                for k in range(nk):
                    act(BK[:, k, 0:wd], src_fn(k, c0, wd), AF.Square, srcreg(k), [("BK", k)])
                b = ps1()
                for k in range(nk):
                    mm(PS[:, b, 0:wd], ONESB[:, :], BK[:, k, 0:wd], k == 0, k == nk - 1,
                       ["ONESB", ("BK", k)], [("PS", b)])
                act(WK[:, 12, 0:wd], PS[:, b, 0:wd], AF.Ln, [("PS", b)], [("WK", 12)], scale=inv_n, bias=EPSB[:, 0:1])
                act(WK[:, 12, 0:wd], WK[:, 12, 0:wd], AF.Exp, [("WK", 12)], [("WK", 12)], scale=-0.5)
                for k in range(nk):
                    stt(out_fn(k, c0, wd), src_fn(k, c0, wd), gain_fn(k), WK[:, 12, 0:wd], ALU.mult, ALU.mult,
                        srcreg(k) + [("WK", 12), "PP"], outreg(k))

        EPSB = sb("EPSB", [128, 2])
        P.op("dve", lambda e: e.memset(EPSB[:, 0:1], EPS), (), ["EPSB"])
        P.op("dve", lambda e: e.memset(EPSB[:, 1:2], 1.0), (), ["EPSB"])

        SQI = [14, 15, 16, 17, 24, 25, 26, 27]

        def norm_p1(blk):
            (c0, wd, vf) = blk
            for k in range(8):
                act(BK[:, SQI[k], 0:wd], H[:, k, c0:c0 + wd], AF.Square, [("H", k)], [("BK", SQI[k])])

        def norm_p2(l, off, blk, b=None):
            (c0, wd, vf) = blk
            if b is None:
                b = ps1()
            for k in range(8):
                mm(PS[:, b, 0:wd], ONESB[:, :], BK[:, SQI[k], 0:wd], k == 0, k == 7,
                   ["ONESB", ("BK", SQI[k])], [("PS", b)])
            act(WK[:, 12, 0:wd], PS[:, b, 0:wd], AF.Ln, [("PS", b)], [("WK", 12)], scale=1.0 / D, bias=EPSB[:, 0:1])
            act(WK[:, 12, 0:wd], WK[:, 12, 0:wd], AF.Exp, [("WK", 12)], [("WK", 12)], scale=-0.5)
            for k in range(8):
                stt(HN[:, k, c0:c0 + wd], H[:, k, c0:c0 + wd], pp(l, off + k), WK[:, 12, 0:wd], ALU.mult, ALU.mult,
                    [("H", k), ("WK", 12), "PP"], [("HN", k, c0)])

        def norm_h(l, off, blocks):
            for blk in blocks:
                norm_p1(blk)
                norm_p2(l, off, blk)

        def lru_chunk(l, blk, c, st, sx, sg):
            (c0, wd, vf) = blk
            li = l * 4 + c
            W = lambda i: WK[:, st * 6 + i, 0:wd]
            wr = lambda i: ("WK", st * 6 + i)
            XSs = XS[:, st, :]
            xr = ("XS", st)
            xcb = BK[:, 8 + st, 0:wd]
            xbr = ("BK", 8 + st)
            hr = ("HIST", li)
            cr = ("CAR", li)
            bx, bg = yield from gps(2)
            for k in range(8):
                mm(PS[:, bx, 0:wd], RING[:, sx[c], k * 128:(k + 1) * 128], HN[:, k, c0:c0 + wd],
                   k == 0, k == 7, [("ring", sx[c]), ("HN", k, c0)], [("PS", bx)])
            for k in range(8):
                mm(PS[:, bg, 0:wd], RING[:, sg[c], k * 128:(k + 1) * 128], HN[:, k, c0:c0 + wd],
                   k == 0, k == 7, [("ring", sg[c]), ("HN", k, c0)], [("PS", bg)])
            yield
            act(XSs[:, 0:3], HIST[:, li, 0:3], AF.Copy, [hr], [xr])
            act(XSs[:, 3:3 + wd], PS[:, bx, 0:wd], AF.Copy, [("PS", bx)], [xr])
            ps_free(bx)
            act(HIST[:, li, 0:3], XSs[:, wd:wd + 3], AF.Copy, [xr], [hr])
            g0 = W(4)
            act(g0, PS[:, bg, 0:wd], AF.Square, [("PS", bg)], [wr(4)])
            yield
            xc = W(0)
            act(xc, XSs[:, 3:3 + wd], AF.Identity, [xr, "PP"], [wr(0)],
                scale=pp(l, CW + c * 4 + 3), bias=pp(l, CB + c))
            ts(g0, g0, 0.044715, 1.0, ALU.mult, ALU.add, [wr(4)], [wr(4)])
            yield
            for tap in (2, 1, 0):
                stt(xc, XSs[:, tap:tap + wd], pp(l, CW + c * 4 + tap), xc, ALU.mult, ALU.add,
                    [xr, wr(0), "PP"], [wr(0)])
                if tap == 2:
                    tt(g0, g0, PS[:, bg, 0:wd], ALU.mult, [wr(4), ("PS", bg)], [wr(4)])
                elif tap == 1:
                    act(g0, g0, AF.Tanh, [wr(4)], [wr(4)], scale=0.7978845608028654)
                else:
                    stt(g0, g0, 1.0, PS[:, bg, 0:wd], ALU.add, ALU.mult, [wr(4), ("PS", bg)], [wr(4)])
                    ps_free(bg)
                yield
            act(xcb, xc, AF.Copy, [wr(0)], [xbr])
            yield
            ba_, bb_ = yield from gps(2)
            gi = (c * 2) * 128
            mm(PS[:, ba_, 0:wd], GATES[:, gi:gi + 128], xcb, True, True, ["GATES", xbr], [("PS", ba_)])
            mm(PS[:, bb_, 0:wd], GATES[:, gi + 128:gi + 256], xcb, True, True, ["GATES", xbr], [("PS", bb_)])
            yield
            t1, t2, t3, t5 = W(1), W(2), W(3), W(5)
            act(t1, PS[:, ba_, 0:wd], AF.Tanh, [("PS", ba_), "DV"], [wr(1)], scale=0.5, bias=dv(l, HBA + c))
            act(t5, PS[:, bb_, 0:wd], AF.Tanh, [("PS", bb_), "DV"], [wr(5)], scale=0.5, bias=dv(l, HBX + c))
            ps_free(ba_, bb_)
            yield
            act(t2, t1, AF.Exp, [wr(1), "DV"], [wr(2)], scale=dv(l, N4 + c), bias=dv(l, N4 + c))
            act(t3, t1, AF.Tanh, [wr(1), "DV"], [wr(3)], scale=dv(l, P4 + c), bias=dv(l, P4 + c))
            stt(t5, t5, 1.0, xc, ALU.add, ALU.mult, [wr(5), wr(0)], [wr(5)])
            yield
            act(t1, t1, AF.Exp, [wr(1), "DV"], [wr(1)], scale=dv(l, N8 + c), bias=dv(l, N8 + c))
            yield
            stt(t1, t1, 1.0, t3, ALU.add, ALU.mult, [wr(1), wr(3)], [wr(1)])
            yield
            act(t1, t1, AF.Ln, [wr(1)], [wr(1)])
            act(t1, t1, AF.Exp, [wr(1)], [wr(1)], scale=0.5)
            yield
            stt(t5, t5, 0.5, t1, ALU.mult, ALU.mult, [wr(5), wr(1)], [wr(5)])
            yield
            if vf:
                P.op("dve", lambda e: e.memset(t1, 0.0), (), [wr(1)])
            P.op("dve", lambda e: e.tensor_tensor_scan(
                out=t1[:, vf:wd], data0=t2[:, vf:wd], data1=t5[:, vf:wd], initial=CAR[:, li:li + 1],
                op0=ALU.mult, op1=ALU.add), [wr(2), wr(5), cr], [wr(1)])
            yield
            act(CAR[:, li:li + 1], t1[:, wd - 1:wd], AF.Copy, [wr(1)], [cr])
            stt(BK[:, 24 + c, 0:wd], t1, 0.5, g0, ALU.mult, ALU.mult, [wr(1), wr(4)], [("BK", 24 + c)])
            yield
            if c == 3:
                for k in range(4):
                    act(BK[:, 14 + k, 0:wd], BK[:, 24 + k, 0:wd], AF.Square, [("BK", 24 + k)], [("BK", 14 + k)])
                yield
                b = yield from gps(1)
                for k in range(4):
                    mm(PS[:, b, 0:wd], ONESB[:, :], BK[:, 14 + k, 0:wd], k == 0, k == 3,
                       ["ONESB", ("BK", 14 + k)], [("PS", b)])
                yield
                act(WK[:, 12, 0:wd], PS[:, b, 0:wd], AF.Ln, [("PS", b)], [("WK", 12)], scale=1.0 / 512,
                    bias=EPSB[:, 0:1])
                ps_free(b)
                act(WK[:, 12, 0:wd], WK[:, 12, 0:wd], AF.Exp, [("WK", 12)], [("WK", 12)], scale=-0.5)
                yield
                for k in range(4):
                    stt(YM[:, k, c0:c0 + wd], BK[:, 24 + k, 0:wd], pp(l, OG + k), WK[:, 12, 0:wd], ALU.mult, ALU.mult,
                        [("BK", 24 + k), ("WK", 12), "PP"], [("YM", k)])
                    yield

        def ret_gate(l, blk, sgr):
            (c0, wd, vf) = blk
            yield "go"
            for hh in range(4):
                b = yield from gps(1)
                for k in range(8):
                    mm(PS[:, b, 0:wd], RING[:, sgr[hh], k * 128:(k + 1) * 128], HN[:, k, c0:c0 + wd],
                       k == 0, k == 7, [("ring", sgr[hh]), ("HN", k, c0)], [("PS", b)])
                yield
                act(BK[:, 10 + hh, 0:wd], PS[:, b, 0:wd], AF.Tanh, [("PS", b)], [("BK", 10 + hh)], scale=0.5)
                yield
                stt(BK[:, 10 + hh, 0:wd], BK[:, 10 + hh, 0:wd], 1.0, PS[:, b, 0:wd], ALU.add, ALU.mult,
                    [("BK", 10 + hh), ("PS", b)], [("BK", 10 + hh)])
                ps_free(b)
                yield

        def ret_tile(l, lc, t0, gt, sq_, sk_, sv_, rs, tn, bc0):
            bi = lambda i: i if (rs == 0 or i < 2) else 16 + i
            B = lambda i: BK[:, bi(i), :]
            br = lambda i: ("BK", bi(i))
            SBr = SB2[:, (tn + 1) % 2, :]
            SBw = SB2[:, tn % 2, :]
            sbr_, sbw_ = ("SB2", (tn + 1) % 2), ("SB2", tn % 2)
            bqk = yield from gps(pair=True)
            for which, s0 in ((0, sq_), (1, sk_)):
                for k in range(8):
                    mm(PS[:, bqk + which, :], HN[:, k, lc:lc + 128], RING[:, s0:s0 + 4, k * 128:(k + 1) * 128],
                       k == 0, k == 7, [("HN", k, bc0)] + [("ring", s0 + j) for j in range(4)], [("PS", bqk + which)])
            bv = yield from gps(1)
            for k in range(8):
                mm(PS[:, bv, :], HN[:, k, lc:lc + 128], RING[:, sv_:sv_ + 4, k * 128:(k + 1) * 128],
                   k == 0, k == 7, [("HN", k, bc0)] + [("ring", sv_ + j) for j in range(4)], [("PS", bv)])
            yield
            act(B(2), PS[:, bv, :], AF.Copy, [("PS", bv)], [br(2)])
            ps_free(bv)
            qk = PS[:, bqk:bqk + 2, :].rearrange("p b (h t j) -> p (b h) t j", h=4, t=2, j=64)
            x1 = qk[:, :, 0, :]
            x2 = qk[:, :, 1, :]
            cs = COS(gt).unsqueeze(1).to_broadcast([128, 8, 64])
            sn = SIN(gt).unsqueeze(1).to_broadcast([128, 8, 64])
            ta = WK[:, 13, :].rearrange("p (h j) -> p h j", j=64)
            tb = WK[:, 14, :].rearrange("p (h j) -> p h j", j=64)
            qr = BK[:, 0:2, :].rearrange("p b (h t j) -> p (b h) t j", h=4, t=2, j=64)
            rqk = [("PS", bqk), ("PS", bqk + 1), "CS"]
            tt(ta, x1, cs, ALU.mult, rqk, [("WK", 13)])
            yield
            tt(tb, x2, sn, ALU.mult, rqk, [("WK", 14)])
            yield
            tt(qr[:, :, 0, :], ta, tb, ALU.subtract, [("WK", 13), ("WK", 14)], [("BK", 0), ("BK", 1)])
            yield
            tt(ta, x1, sn, ALU.mult, rqk, [("WK", 13)])
            yield
            tt(tb, x2, cs, ALU.mult, rqk, [("WK", 14)])
            ps_free(bqk, bqk + 1)
            yield
            tt(qr[:, :, 1, :], ta, tb, ALU.add, [("WK", 13), ("WK", 14)], [("BK", 0), ("BK", 1)])
            yield
            tt(B(3), BK[:, 1, :], ZETA, ALU.mult, [("BK", 1), "CST"], [br(3)])
            pqk = yield from gpsb()
            for hh in range(4):
                tr(PSB[:, pqk, hh * 128:(hh + 1) * 128], BK[:, 0, hh * 128:(hh + 1) * 128], IDB[:, :],
                   [("BK", 0), "IDB"], [("PSB", pqk)])
            for hh in range(4):
                tr(PSB[:, pqk, 512 + hh * 128:512 + (hh + 1) * 128], BK[:, 1, hh * 128:(hh + 1) * 128], IDB[:, :],
                   [("BK", 1), "IDB"], [("PSB", pqk)])
            yield
            bS = yield from gps(1)
            for hh in range(4):
                sl = slice(hh * 128, (hh + 1) * 128)
                mm(PS[:, bS, sl], B(3)[:, sl], B(2)[:, sl], True, True, [br(3), br(2)], [("PS", bS)])
            tt(B(4), PSB[:, pqk, 0:512], XI, ALU.mult, [("PSB", pqk), "CST"], [br(4)])
            act(B(5), PSB[:, pqk, 512:1024], AF.Copy, [("PSB", pqk), br(4)], [br(5)])
            psb_free(pqk)
            yield
            for hh in range(4):
                sl = slice(hh * 128, (hh + 1) * 128)
                stt(SF[:, l, sl], SF[:, l, sl], GAM[hh] ** 128, PS[:, bS, sl], ALU.mult, ALU.add,
                    ["SF", ("PS", bS)], ["SF"])
            ps_free(bS)
            yield
            act(SBw, SF[:, l, :], AF.Copy, ["SF"], [sbw_])
            yield "go"
            bs = yield from gps(1)
            for hh in range(4):
                sl = slice(hh * 128, (hh + 1) * 128)
                mm(PS[:, bs, sl], B(5)[:, sl], B(4)[:, sl], True, True, [br(5), br(4)], [("PS", bs)])
            yield
            tt(B(6), PS[:, bs, :], MASK, ALU.mult, [("PS", bs), "CST"], [br(6)])
            ps_free(bs)
            yield
            by = yield from gps(1)
            for hh in range(4):
                sl = slice(hh * 128, (hh + 1) * 128)
                mm(PS[:, by, sl], B(6)[:, sl], B(2)[:, sl], True, False, [br(6), br(2)], [("PS", by)])
                mm(PS[:, by, sl], B(4)[:, sl], SBr[:, sl], False, True, [br(4), sbr_], [("PS", by)])
            yield
            for hh in range(4):
                sl = slice(hh * 128, (hh + 1) * 128)
                P.op("dve", lambda e, hh=hh, sl=sl: e.bn_stats(out=STAT[:, rs, hh, 0:6], in_=PS[:, by, sl]),
                     [("PS", by)], [("STAT", rs, hh)])
                yield
            for hh in range(4):
                P.op("dve", lambda e, hh=hh: e.bn_aggr(out=MV[:, rs, hh, :], in_=STAT[:, rs, hh, 0:6]),
                     [("STAT", rs, hh)], [("MV", rs)])
            yield
            act(RSTD[:, rs * 4:rs * 4 + 4], MV[:, rs, :, 1], AF.Ln, [("MV", rs)], [("RSTD", rs)], bias=EPSB[:, 0:1])
            act(RSTD[:, rs * 4:rs * 4 + 4], RSTD[:, rs * 4:rs * 4 + 4], AF.Exp, [("RSTD", rs)], [("RSTD", rs)],
                scale=-0.5)
            yield
            for hh in range(4):
                sl = slice(hh * 128, (hh + 1) * 128)
                ts(B(7)[:, sl], PS[:, by, sl], MV[:, rs, hh, 0:1], RSTD[:, rs * 4 + hh:rs * 4 + hh + 1],
                   ALU.subtract, ALU.mult, [("PS", by), ("MV", rs), ("RSTD", rs)], [br(7)])
            ps_free(by)
            yield
            py = yield from gpsb()
            for hh in range(4):
                tr(PSB[:, py, hh * 128:(hh + 1) * 128], B(7)[:, hh * 128:(hh + 1) * 128], IDB[:, :],
                   [br(7), "IDB"], [("PSB", py)])
            yield
            for hh in range(4):
                stt(YM[:, 4 + hh, lc:lc + 128], PSB[:, py, hh * 128:(hh + 1) * 128], dv(l, HRG + hh),
                    BK[:, 10 + hh, t0:t0 + 128], ALU.mult, ALU.mult,
                    [("PSB", py), "DV", ("BK", 10 + hh)], [("YM", 4 + hh)])
                if hh == 3:
                    psb_free(py)
                yield

        class Stream:
            def __init__(self, groups, width, gated=False):
                self.groups = groups
                self.gi = 0
                self.cur = []
                self.active = []
                self.width = width
                self.gated = gated
                self.open = True

            def done_groups(self):
                return self.gi - 1 if (self.active or self.cur) else self.gi

            def step(self):
                if not self.active and not self.cur:
                    if self.gi >= len(self.groups):
                        return False
                    self.cur = list(self.groups[self.gi])
                    self.gi += 1
                    self.open = True
                while len(self.active) < self.width and self.cur and (self.open or not self.gated):
                    self.active.append(self.cur.pop(0))
                    self.open = False
                for g in list(self.active):
                    try:
                        r = next(g)
                        if r == "go":
                            self.open = True
                    except StopIteration:
                        self.active.remove(g)
                        if g is getattr(self, "last", None):
                            pass
                return True

        def mixer_phase(l, blocks, hc0):
            assert sorted(fifo) == list(range(RS))
            sq_ = take_run4()
            sk_ = take_run4()
            sv_ = take_run4()
            sx = [None] * 4
            sg = [None] * 4
            sx[0], sg[0] = take(2)
            load_cols(win_d, l, 0, sx[0])
            load_cols(win_d, l, 512, sg[0])
            P.dma("pool", lambda e: e.dma_start(out=GATES[:, :], in_=gates_d[:, l * 1024:(l + 1) * 1024]),
                  SEM_SETUP + 3, (), ["GATES"])
            sgr = take(4)
            for c in range(4):
                load_cols(win_d, l, 2560 + c * 128, sgr[c])
            for base, s0 in ((1024, sq_), (1536, sk_), (2048, sv_)):
                for c in range(4):
                    load_cols(win_d, l, base + c * 128, s0 + c)
            rest = take(6)
            for c in range(1, 4):
                sx[c] = rest[2 * (c - 1)]
                sg[c] = rest[2 * (c - 1) + 1]
                load_cols(win_d, l, c * 128, sx[c])
                load_cols(win_d, l, 512 + c * 128, sg[c])
            assert not fifo
            act(SB2[:, 1, :], SF[:, l, :], AF.Copy, ["SF"], [("SB2", 1)])
            lru_groups = []
            ui = 0
            for blk in blocks:
                g = []
                for c in range(4):
                    g.append(lru_chunk(l, blk, c, ui % 2, sx, sg))
                    ui += 1
                lru_groups.append(g)
            ret_groups = []
            tn = 0
            for blk in blocks:
                (c0, wd, vf) = blk
                g = [ret_gate(l, blk, sgr)]
                for t0 in range(0, wd, 128):
                    lc = c0 + t0
                    g.append(ret_tile(l, lc, t0, (hc0 + lc) // 128, sq_, sk_, sv_, tn % 2, tn, c0))
                    tn += 1
                ret_groups.append(g)
            s_ret = Stream(ret_groups, 2, gated=True)
            s_lru = Stream(lru_groups, 2)

            def out_stream():
                while s_lru.done_groups() < len(blocks):
                    yield
                so = sx + sg
                for dc in range(8):
                    load_cols(wout_d, l, dc * 128, so[dc])
                for i, blk in enumerate(blocks):
                    (c0, wd, vf) = blk
                    while s_ret.done_groups() <= i:
                        yield
                    for dc in range(8):
                        b = yield from gps(1)
                        for k in range(8):
                            mm(PS[:, b, 0:wd], RING[:, so[dc], k * 128:(k + 1) * 128], YM[:, k, c0:c0 + wd],
                               k == 0, k == 7, [("ring", so[dc]), ("YM", k)], [("PS", b)])
                        yield
                        resid_add(b, dc, c0, wd, vf)
                        ps_free(b)
                        yield
                    norm_p1(blk)
                    yield
                    b = yield from gps(1)
                    norm_p2(l, NF, blk, b)
                    ps_free(b)
                    yield

            streams = [s_ret, s_lru, Stream([[out_stream()]], 1)]
            alive = True
            while alive:
                alive = False
                for st in streams:
                    if st.step():
                        alive = True
            release(sgr + [sq_ + j for j in range(4)] + [sk_ + j for j in range(4)] + [sv_ + j for j in range(4)])
            release(sx + sg)

        def resid_add(b, dc, c0, wd, vf):
            tt(H[:, dc, c0 + vf:c0 + wd], PS[:, b, vf:wd], H[:, dc, c0 + vf:c0 + wd], ALU.add,
               [("PS", b), ("H", dc)], [("H", dc)])

        def out_phase(l, blocks):
            so = ring_alloc(8)
            for dc in range(8):
                load_cols(wout_d, l, dc * 128, so + dc)
            for i, (c0, wd, vf) in enumerate(blocks):
                for dc in range(8):
                    b = ps1()
                    for k in range(8):
                        mm(PS[:, b, 0:wd], RING[:, so + dc, k * 128:(k + 1) * 128], YM[:, k, c0:c0 + wd],
                           k == 0, k == 7, [("ring", so + dc), ("YM", k)], [("PS", b)])
                    resid_add(b, dc, c0, wd, vf)
                if i > 0:
                    norm_p2(l, NF, blocks[i - 1])
                norm_p1(blocks[i])
            norm_p2(l, NF, blocks[-1])

        def ffn_phase(l, blocks, next_norm):
            j0 = 0
            for gidx, ng in enumerate(FFG):
                sgs, sus, sds = [], [], []
                for j in range(j0, j0 + ng):
                    s3 = take(3)
                    load_cols(wg_d, l, j * 128, s3[0])
                    load_cols(wu_d, l, j * 128, s3[1])
                    load_rows(wd_d, l, j * 128, s3[2])
                    sgs.append(s3[0])
                    sus.append(s3[1])
                    sds.append(s3[2])
                for bi, (c0, wd, vf) in enumerate(blocks):
                    for jj in range(ng):
                        bg = ps1()
                        bu = ps1()
                        for k in range(8):
                            mm(PS[:, bg, 0:wd], RING[:, sgs[jj], k * 128:(k + 1) * 128], HN[:, k, c0:c0 + wd],
                               k == 0, k == 7, [("ring", sgs[jj]), ("HN", k, c0)], [("PS", bg)])
                        for k in range(8):
                            mm(PS[:, bu, 0:wd], RING[:, sus[jj], k * 128:(k + 1) * 128], HN[:, k, c0:c0 + wd],
                               k == 0, k == 7, [("ring", sus[jj]), ("HN", k, c0)], [("PS", bu)])
                        wi = jj % 2
                        t = WK[:, wi, 0:wd]
                        act(t, PS[:, bg, 0:wd], AF.Silu, [("PS", bg)], [("WK", wi)])
                        ai = (bi % 2) * 4 + jj
                        tt(BK[:, ai, 0:wd], t, PS[:, bu, 0:wd], ALU.mult, [("WK", wi), ("PS", bu)], [("BK", ai)])
                    for dc in range(8):
                        b = ps1()
                        for jj in range(ng):
                            ai = (bi % 2) * 4 + jj
                            mm(PS[:, b, 0:wd], RING[:, sds[jj], dc * 128:(dc + 1) * 128], BK[:, ai, 0:wd],
                               jj == 0, jj == ng - 1, [("ring", sds[jj]), ("BK", ai)], [("PS", b)])
                        resid_add(b, dc, c0, wd, vf)
                    if next_norm and gidx == len(FFG) - 1:
                        if bi > 0:
                            norm_p2(l + 1, NM, blocks[bi - 1])
                        norm_p1(blocks[bi])
                if next_norm and gidx == len(FFG) - 1:
                    norm_p2(l + 1, NM, blocks[-1])
                release(sgs + sus + sds)
                j0 += ng

        for hi, (t_lo, t_hi) in enumerate(HALVES):
            hc0 = t_lo * 128
            ntile = t_hi - t_lo
            if hi == 0:
                blocks = [(0, 384, PADF), (384, 384, 0), (768, 384, 0)]
            else:
                blocks = [(0, 512, 0), (512, 512, 0)]
            tlo[0] = t_lo
            P.dma("sp", lambda e, hi=hi: e.dma_start(out=CS[:, :, :].rearrange("p a f -> p (a f)"),
                                                      in_=cs_d[:, hi * 1152:(hi + 1) * 1152]),
                  SEM_SETUP + 2, (), ["CS"])
            for ti in range(ntile):
                gt = t_lo + ti
                wb = (ti % 2) * 2
                xst = WK[:, wb:wb + 2, :].rearrange("p a f -> p (a f)")
                regs = [("WK", wb), ("WK", wb + 1)]
                if gt == 0:
                    P.op("dve", lambda e, xst=xst: e.memset(xst, 0.0), (), regs)
                    P.dma("sp", lambda e, xst=xst: e.dma_start(out=xst[PADF:128, :], in_=meta_d), SEM_XST[ti % 2],
                          (), regs)
                else:
                    r0 = (gt - 1) * 128
                    P.dma("sp", lambda e, xst=xst, r0=r0: e.dma_start(out=xst, in_=x_d[r0:r0 + 128, :]),
                          SEM_XST[ti % 2], (), regs)
                b = ps2()
                for k in range(8):
                    bb = b + k // 4
                    tr(PS[:, bb, (k % 4) * 128:(k % 4 + 1) * 128], xst[:, k * 128:(k + 1) * 128], IDENT,
                       regs + ["CST"], [("PS", bb)])
                for k in range(8):
                    bb = b + k // 4
                    act(H[:, k, ti * 128:(ti + 1) * 128], PS[:, bb, (k % 4) * 128:(k % 4 + 1) * 128], AF.Copy,
                        [("PS", bb)], [("H", k)])
            for l in range(DEPTH):
                if l == 0:
                    norm_h(l, NM, blocks)
                mixer_phase(l, blocks, hc0)
                if DBG is not None and DBG == (l, hi):
                    P.dma("sp", lambda e: e.dma_start(out=dbg_d, in_=YM[:, :, :].rearrange("p a f -> p (a f)")),
                          SEM_SETUP + 1, [("YM", k) for k in range(8)], ())
                ffn_phase(l, blocks, l < DEPTH - 1)
            P.dma("sp", lambda e: e.dma_start(out=GFIN, in_=gfin_d), SEM_SETUP + 3 + 1, (), [("WK", 8), ("WK", 9)])
            for ti in range(ntile):
                gt = t_lo + ti
                if gt == 0:
                    continue
                b = ps2()
                for k in range(8):
                    bb = b + k // 4
                    tr(PS[:, bb, (k % 4) * 128:(k % 4 + 1) * 128], H[:, k, ti * 128:(ti + 1) * 128], IDENT,
                       [("H", k), "CST"], [("PS", bb)])
                wb = (ti % 2) * 2
                ho = WK[:, wb:wb + 2, :].rearrange("p a f -> p (a f)")
                regs = [("WK", wb), ("WK", wb + 1)]
                for a in range(2):
                    act(ho[:, a * 512:(a + 1) * 512], PS[:, b + a, :], AF.Copy, [("PS", b + a)], regs)
                ob = 4 + (ti % 2) * 2
                ot = WK[:, ob:ob + 2, :].rearrange("p a f -> p (a f)")
                oregs = [("WK", ob), ("WK", ob + 1)]
                act(ot, ho, AF.Square, regs, oregs + ["RSTD"], accum=RSTD[:, 8:9])
                act(RSTD[:, 8:9], RSTD[:, 8:9], AF.Ln, ["RSTD"], ["RSTD"], scale=1.0 / D, bias=EPSB[:, 0:1])
                act(RSTD[:, 8:9], RSTD[:, 8:9], AF.Exp, ["RSTD"], ["RSTD"], scale=-0.5)
                stt(ot, ho, RSTD[:, 8:9], GFIN, ALU.mult, ALU.mult, regs + ["RSTD", ("WK", 8), ("WK", 9)], oregs)
                r0 = (gt - 1) * 128
                P.dma("sp", lambda e, ot=ot, r0=r0: e.dma_start(out=out_d[r0:r0 + 128, :], in_=ot),
                      SEM_OUT[ti % 2], oregs, ())
        P.op("sp", None, (), [("WK", i) for i in range(4, 8)])

        P.finalize()
        with nc.Block() as block:
            @block.tensor
            def _(e):
                P.emit("pe", e, esems, dsems)

            @block.scalar
            def _(e):
                P.emit("act", e, esems, dsems)

            @block.vector
            def _(e):
                P.emit("dve", e, esems, dsems)

            @block.gpsimd
            def _(e):
                P.emit("pool", e, esems, dsems)

            @block.sync
            def _(e):
                P.emit("sp", e, esems, dsems)
    return nc


def _host_tables():
    pos = np.maximum(np.arange(TP, dtype=np.float32) - PADF, 0.0).astype(np.float32)
    half = 64
    inv = (np.float32(10000.0) ** (-np.arange(half, dtype=np.float32) / np.float32(half))).astype(np.float32)
    ang = (pos[:, None] * inv[None, :]).astype(np.float32)
    cos = np.cos(ang).astype(np.float32).reshape(NT, 128, 64).transpose(1, 0, 2)
    sin = np.sin(ang).astype(np.float32).reshape(NT, 128, 64).transpose(1, 0, 2)
    cs = np.zeros((128, 2, 2, 9, 64), np.float32)
    for hi, (a, b) in enumerate(HALVES):
        cs[:, hi, 0, 0:b - a] = cos[:, a:b]
        cs[:, hi, 1, 0:b - a] = sin[:, a:b]
    idx = np.arange(128, dtype=np.float64)
    mask = np.zeros((128, 4, 128), np.float32)
    xi = np.zeros((128, 4, 128), np.float32)
    zeta = np.zeros((128, 4, 128), np.float32)
    for h in range(4):
        g = GAM[h]
        m = np.where(idx[None, :] >= idx[:, None], (g ** (-(idx + 1.0)))[:, None], 0.0)
        mask[:, h, :] = m
        xi[:, h, :] = ((g ** (idx + 1.0)) * (128.0 ** -0.5))[None, :]
        zeta[:, h, :] = (g ** (127.0 - idx))[:, None]
    ident = np.eye(128, dtype=np.float32)
    cst = np.concatenate([ident, mask.reshape(128, 512), xi.reshape(128, 512), zeta.reshape(128, 512)], axis=1)
    return np.ascontiguousarray(cst.astype(np.float32)), np.ascontiguousarray(cs.reshape(128, -1))


_NC_CACHE = {}


def kernel(x, meta_tokens, norm_mix, w_in, conv_w, conv_b, gate_a_w, gate_a_b, gate_x_w, gate_x_b,
           lru_lambda, lru_out_norm, ret_out_norm, w_out, norm_ffn, w_gate, w_up, w_down, norm_final):
    f = lambda a: np.ascontiguousarray(np.asarray(a, dtype=np.float32))
    x = f(x)
    pp = np.zeros((128, DEPTH * PL), np.float32)
    gates = np.zeros((128, DEPTH * 4 * 2 * 128), np.float32)
    gaw, gxw = f(gate_a_w), f(gate_x_w)
    for l in range(DEPTH):
        o = l * PL
        pp[:, o + NM:o + NM + 8] = f(norm_mix)[l].reshape(8, 128).T
        pp[:, o + NF:o + NF + 8] = f(norm_ffn)[l].reshape(8, 128).T
        cw = f(conv_w)[l]
        for c in range(4):
            pp[:, o + CW + c * 4:o + CW + c * 4 + 4] = cw[:, c * 128:(c + 1) * 128].T
        pp[:, o + CB:o + CB + 4] = f(conv_b)[l].reshape(4, 128).T
        pp[:, o + BA:o + BA + 4] = f(gate_a_b)[l].reshape(4, 128).T
        pp[:, o + BX:o + BX + 4] = f(gate_x_b)[l].reshape(4, 128).T
        pp[:, o + LAM:o + LAM + 4] = f(lru_lambda)[l].reshape(4, 128).T
        pp[:, o + OG:o + OG + 4] = f(lru_out_norm)[l].reshape(4, 128).T
        pp[:, o + RG:o + RG + 4] = f(ret_out_norm)[l].reshape(4, 128).T
        for c in range(4):
            for wi, gw in enumerate((gaw, gxw)):
                base = ((l * 4 + c) * 2 + wi) * 128
                gates[0:64, base:base + 64] = gw[l, 2 * c]
                gates[64:128, base + 64:base + 128] = gw[l, 2 * c + 1]
    cst, cs = _host_tables()
    gfin = np.ascontiguousarray(np.broadcast_to(f(norm_final)[None, :], (128, D)))
    if "nc" not in _NC_CACHE:
        _NC_CACHE["nc"] = build_nc()
    nc = _NC_CACHE["nc"]
    shared = {"meta": f(meta_tokens), "w_in": f(w_in), "w_out": f(w_out), "w_gate": f(w_gate), "w_up": f(w_up),
              "w_down": f(w_down), "gates": gates, "pp": pp, "cst": cst, "cs": cs, "gfin": gfin}
    in_maps = [dict(shared, x=x[b]) for b in range(8)]
    res = run_bass_kernel_spmd(nc, in_maps, core_ids=list(range(8)))
    if DBG is not None:
        DBG_OUT["dbg"] = np.asarray(res.results[0]["dbg"])
    return np.stack([np.asarray(r["out"], dtype=np.float32) for r in res.results], axis=0)
```
